# Optimizing a Trainium2 kernel written in Bass

```python
import math
import jax, jax.numpy as jnp
from jax import lax
import numpy as np

D_MODEL = 2048
BATCH = 2
SEQ = 8192
DEPTH = 4

N_META = 16
MIX_WIDTH = D_MODEL
N_EVEN = (DEPTH + 1) // 2
N_ODD = DEPTH // 2
NORM_EPS = 1e-6
LRU_WIDTH = MIX_WIDTH // 2
LRU_BLOCKS = 8
LRU_BLOCK = LRU_WIDTH // LRU_BLOCKS
CONV_WIDTH = 4
LRU_C = 8.0
DA_WIDTH = MIX_WIDTH - LRU_WIDTH
DA_HEADS = 8
DA_HEAD_DIM = DA_WIDTH // DA_HEADS
DA_QK_DIM = DA_HEAD_DIM // 2
Q_BLOCK = 128
EVEN_IN = 2 * LRU_WIDTH + 3 * DA_WIDTH
S5_WIDTH = MIX_WIDTH // 2
S5_GROUP = 16
S5_GROUPS = S5_WIDTH // S5_GROUP
S5_STATE = 64
RW_WIDTH = MIX_WIDTH - S5_WIDTH
RW_HEAD_DIM = 64
RW_HEADS = RW_WIDTH // RW_HEAD_DIM
RW_DECAY_LORA = max(32, int(round(1.8 * D_MODEL ** 0.5 / 32)) * 32)
RW_AAA_LORA = RW_DECAY_LORA
RW_GATE_LORA = max(32, int(round(0.6 * D_MODEL ** 0.8 / 32)) * 32)
RW_IN = 3 * RW_WIDTH + RW_DECAY_LORA + RW_AAA_LORA + RW_GATE_LORA
RW_GN_EPS = 64e-5
ODD_IN = S5_WIDTH + RW_IN
FFN_HIDDEN = ((-(-8 * D_MODEL // 3) + 255) // 256) * 256

kernel_name = 'hybrid_lru_diffattn_s5_rwkv7_trunk'


def rmsnorm(x, g, eps=NORM_EPS):
    x32 = x.astype(jnp.float32)
    y = x32 * lax.rsqrt(jnp.mean(x32 * x32, axis=-1, keepdims=True) + eps)
    return (y * g.astype(jnp.float32)).astype(x.dtype)


def split_cols(z, sizes):
    out, start = [], 0
    for s in sizes:
        out.append(z[..., start:start + s])
        start += s
    return out


def causal_depthwise_conv(u, w, b):
    T = u.shape[1]
    up = jnp.pad(u, ((0, 0), (CONV_WIDTH - 1, 0), (0, 0)))
    out = b
    for j in range(CONV_WIDTH):
        out = out + up[:, j:j + T] * w[j]
    return out


def linear_scan(a, b):
    def combine(l, r):
        al, bl = l
        ar, br = r
        return ar * al, ar * bl + br
    _, h = lax.associative_scan(combine, (a, b), axis=1)
    return h


def rg_lru(u, w_a, b_a, w_x, b_x, lam):
    Bb, T, W = u.shape
    u = u.astype(jnp.float32)
    ub = u.reshape(Bb, T, LRU_BLOCKS, LRU_BLOCK)
    def gate(w, b):
        z = jnp.einsum('btnc,ncd->btnd', ub, w.astype(jnp.float32)).reshape(Bb, T, W)
        return jax.nn.sigmoid(z + b.astype(jnp.float32))
    r = gate(w_a, b_a)
    i = gate(w_x, b_x)
    log_a = -LRU_C * r * jax.nn.softplus(-lam.astype(jnp.float32))
    a = jnp.exp(log_a)
    b = jnp.sqrt(-jnp.expm1(2.0 * log_a)) * (i * u)
    return linear_scan(a, b)


def diff_attention(q, k, v, lq1, lk1, lq2, lk2, subln_g, lambda_init):
    Bb, T, _ = q.shape
    H, d, E = DA_HEADS, DA_QK_DIM, DA_HEAD_DIM
    f32 = jnp.float32
    q = q.astype(f32).reshape(Bb, T, H, 2, d) * (d ** -0.5)
    k = k.astype(f32).reshape(Bb, T, H, 2, d)
    v = v.astype(f32).reshape(Bb, T, H, E)
    lam = (jnp.exp(jnp.sum(lq1.astype(f32) * lk1.astype(f32)))
           - jnp.exp(jnp.sum(lq2.astype(f32) * lk2.astype(f32))) + lambda_init)
    slopes = 2.0 ** (-8.0 * jnp.arange(1, H + 1, dtype=f32) / H)
    k_pos = jnp.arange(T, dtype=jnp.int32)

    def attend(args):
        qb, q_pos = args
        s = jnp.einsum('bqhmd,bkhmd->mbhqk', qb, k)
        dist = (q_pos[:, None] - k_pos[None, :]).astype(f32)
        s = s - slopes[:, None, None] * dist
        s = jnp.where(k_pos[None, :] <= q_pos[:, None], s, -jnp.inf)
        p = jax.nn.softmax(s, axis=-1)
        attn = p[0] - lam * p[1]
        return jnp.einsum('bhqk,bkhe->bqhe', attn, v)

    o_meta = attend((q[:, :N_META], jnp.arange(N_META, dtype=jnp.int32)))
    n_blk = (T - N_META) // Q_BLOCK
    q_real = q[:, N_META:].reshape(Bb, n_blk, Q_BLOCK, H, 2, d).transpose(1, 0, 2, 3, 4, 5)
    pos = (N_META + jnp.arange(T - N_META, dtype=jnp.int32)).reshape(n_blk, Q_BLOCK)
    o_real = lax.map(attend, (q_real, pos))
    o_real = o_real.transpose(1, 0, 2, 3, 4).reshape(Bb, T - N_META, H, E)
    o = jnp.concatenate([o_meta, o_real], axis=1)
    o = rmsnorm(o, subln_g, 1e-5) * (1.0 - lambda_init)
    return o.reshape(Bb, T, H * E)


def s5_layer(u, lam_re, lam_im, log_dt, b_re, b_im, c_re, c_im, d_skip):
    Bb, T, W = u.shape
    f32 = jnp.float32
    u = u.astype(f32)
    lr, li = lam_re.astype(f32), lam_im.astype(f32)
    dt = jnp.exp(log_dt.astype(f32))[:, None]
    mag = jnp.exp(lr * dt)
    ar, ai = mag * jnp.cos(li * dt), mag * jnp.sin(li * dt)
    den = lr * lr + li * li
    cr = ((ar - 1.0) * lr + ai * li) / den
    ci = (ai * lr - (ar - 1.0) * li) / den
    br, bi = b_re.astype(f32), b_im.astype(f32)
    bbr = cr[..., None] * br - ci[..., None] * bi
    bbi = cr[..., None] * bi + ci[..., None] * br
    ug = u.reshape(Bb, T, S5_GROUPS, S5_GROUP)
    bu_r = jnp.einsum('btgc,gpc->btgp', ug, bbr)
    bu_i = jnp.einsum('btgc,gpc->btgp', ug, bbi)
    a_r = jnp.broadcast_to(ar, bu_r.shape)
    a_i = jnp.broadcast_to(ai, bu_r.shape)

    def combine(l, r):
        alr, ali, blr, bli = l
        arr, ari, brr, bri = r
        return (arr * alr - ari * ali, arr * ali + ari * alr,
                arr * blr - ari * bli + brr, arr * bli + ari * blr + bri)

    _, _, hr, hi = lax.associative_scan(combine, (a_r, a_i, bu_r, bu_i), axis=1)
    y = (jnp.einsum('btgp,gcp->btgc', hr, c_re.astype(f32))
         - jnp.einsum('btgp,gcp->btgc', hi, c_im.astype(f32)))
    return y.reshape(Bb, T, W) + d_skip.astype(f32) * u


def rwkv7_time_mix(dcols, mu, w0, w2, a0, a2, g2, k_k, k_a, r_k, ln_w, ln_b):
    Bb, T, _ = dcols.shape
    H, N = RW_HEADS, RW_HEAD_DIM
    f32 = jnp.float32
    dcols = dcols.astype(f32)
    d_prev = jnp.pad(dcols, ((0, 0), (1, 0), (0, 0)))[:, :-1]
    dcols = dcols + (d_prev - dcols) * mu.astype(f32)
    r, k, v, wl, al, gl = split_cols(dcols, (RW_WIDTH, RW_WIDTH, RW_WIDTH, RW_DECAY_LORA, RW_AAA_LORA, RW_GATE_LORA))
    w = -jax.nn.softplus(-(w0.astype(f32) + jnp.tanh(wl) @ w2.astype(f32))) - 0.5
    decay = jnp.exp(-jnp.exp(w))
    a = jax.nn.sigmoid(a0.astype(f32) + al @ a2.astype(f32))
    g = jax.nn.sigmoid(gl) @ g2.astype(f32)
    heads = lambda t: t.reshape(Bb, T, H, N)
    kk = heads(k * k_k.astype(f32))
    kk = kk / jnp.maximum(jnp.sqrt(jnp.sum(kk * kk, axis=-1, keepdims=True)), 1e-12)
    k = k * (1.0 + (a - 1.0) * k_a.astype(f32))

    def step(S, inp):
        r_t, w_t, k_t, v_t, kk_t, a_t = inp
        sa = jnp.einsum('bhij,bhj->bhi', S, -kk_t)
        S = (S * w_t[:, :, None, :] + sa[..., None] * (kk_t * a_t)[:, :, None, :]
             + v_t[..., None] * k_t[:, :, None, :])
        return S, jnp.einsum('bhij,bhj->bhi', S, r_t)

    xs = (heads(r), heads(decay), heads(k), heads(v), kk, heads(a))
    xs = tuple(jnp.moveaxis(t, 1, 0) for t in xs)
    S0 = jnp.zeros((Bb, H, N, N), f32)
    _, y = lax.scan(step, S0, xs)
    y = jnp.moveaxis(y, 0, 1)
    mean = jnp.mean(y, axis=-1, keepdims=True)
    var = jnp.mean(jnp.square(y - mean), axis=-1, keepdims=True)
    y = ((y - mean) * lax.rsqrt(var + RW_GN_EPS)).reshape(Bb, T, RW_WIDTH)
    y = y * ln_w.astype(f32) + ln_b.astype(f32)
    bonus = jnp.sum(heads(r * k * r_k.astype(f32)), axis=-1, keepdims=True) * heads(v)
    return (y + bonus.reshape(Bb, T, RW_WIDTH)) * g


def swiglu(h, w_gate, w_up, w_down):
    return (jax.nn.silu(h @ w_gate) * (h @ w_up)) @ w_down


def setup_inputs(seed: int = 0) -> dict:
    key = jax.random.key(seed)
    ks = iter(jax.random.split(key, 64))
    f32 = jnp.float32
    def nrm(shape, scale):
        return jax.random.normal(next(ks), shape, f32) * scale
    def unif(shape, lo, hi):
        return jax.random.uniform(next(ks), shape, f32, lo, hi)
    NE, NO = N_EVEN, N_ODD
    lru_s = unif((NE, LRU_WIDTH), 0.9, 0.999) ** (1.0 / LRU_C)
    lam_im = jnp.pi * jnp.arange(S5_STATE, dtype=f32) + nrm((NO, S5_GROUPS, S5_STATE), 0.01)
    return {
        'x': nrm((BATCH, SEQ, D_MODEL), 1.0),
        'meta_tokens': nrm((N_META, D_MODEL), 1.0),
        'norm_mix_g': 1.0 + nrm((DEPTH, D_MODEL), 0.02),
        'norm_ffn_g': 1.0 + nrm((DEPTH, D_MODEL), 0.02),
        'final_norm_g': 1.0 + nrm((D_MODEL,), 0.02),
        'ev_w_in': nrm((NE, D_MODEL, EVEN_IN), D_MODEL ** -0.5),
        'ev_conv_w': nrm((NE, CONV_WIDTH, LRU_WIDTH), CONV_WIDTH ** -0.5),
        'ev_conv_b': nrm((NE, LRU_WIDTH), 0.02),
        'ev_lru_wa': nrm((NE, LRU_BLOCKS, LRU_BLOCK, LRU_BLOCK), LRU_BLOCK ** -0.5),
        'ev_lru_ba': nrm((NE, LRU_WIDTH), 0.02),
        'ev_lru_wx': nrm((NE, LRU_BLOCKS, LRU_BLOCK, LRU_BLOCK), LRU_BLOCK ** -0.5),
        'ev_lru_bx': nrm((NE, LRU_WIDTH), 0.02),
        'ev_lru_lambda': jnp.log(lru_s) - jnp.log1p(-lru_s),
        'ev_lq1': nrm((NE, DA_QK_DIM), 0.1),
        'ev_lk1': nrm((NE, DA_QK_DIM), 0.1),
        'ev_lq2': nrm((NE, DA_QK_DIM), 0.1),
        'ev_lk2': nrm((NE, DA_QK_DIM), 0.1),
        'ev_subln_g': 1.0 + nrm((NE, DA_HEAD_DIM), 0.02),
        'ev_w_out': nrm((NE, MIX_WIDTH, D_MODEL), MIX_WIDTH ** -0.5),
        'od_w_in': nrm((NO, D_MODEL, ODD_IN), D_MODEL ** -0.5),
        'od_s5_lam_re': -0.5 + nrm((NO, S5_GROUPS, S5_STATE), 0.01),
        'od_s5_lam_im': lam_im,
        'od_s5_log_dt': unif((NO, S5_GROUPS), math.log(0.001), math.log(0.1)),
        'od_s5_b_re': nrm((NO, S5_GROUPS, S5_STATE, S5_GROUP), (2 * S5_GROUP) ** -0.5),
        'od_s5_b_im': nrm((NO, S5_GROUPS, S5_STATE, S5_GROUP), (2 * S5_GROUP) ** -0.5),
        'od_s5_c_re': nrm((NO, S5_GROUPS, S5_GROUP, S5_STATE), S5_STATE ** -0.5),
        'od_s5_c_im': nrm((NO, S5_GROUPS, S5_GROUP, S5_STATE), S5_STATE ** -0.5),
        'od_s5_d': nrm((NO, S5_WIDTH), 1.0),
        'od_glu_w': nrm((NO, S5_WIDTH, S5_WIDTH), S5_WIDTH ** -0.5),
        'od_glu_b': nrm((NO, S5_WIDTH), 0.02),
        'od_rw_mu': unif((NO, RW_IN), 0.0, 1.0),
        'od_rw_w0': unif((NO, RW_WIDTH), -4.0, 0.0),
        'od_rw_w2': nrm((NO, RW_DECAY_LORA, RW_WIDTH), 0.5 * RW_DECAY_LORA ** -0.5),
        'od_rw_a0': nrm((NO, RW_WIDTH), 0.1),
        'od_rw_a2': nrm((NO, RW_AAA_LORA, RW_WIDTH), 0.5 * RW_AAA_LORA ** -0.5),
        'od_rw_g2': nrm((NO, RW_GATE_LORA, RW_WIDTH), RW_GATE_LORA ** -0.5),
        'od_rw_kk': 0.85 + nrm((NO, RW_WIDTH), 0.02),
        'od_rw_ka': 1.0 + nrm((NO, RW_WIDTH), 0.02),
        'od_rw_rk': nrm((NO, RW_WIDTH), 0.1),
        'od_rw_ln_w': 1.0 + nrm((NO, RW_WIDTH), 0.02),
        'od_rw_ln_b': nrm((NO, RW_WIDTH), 0.02),
        'od_w_out': nrm((NO, MIX_WIDTH, D_MODEL), MIX_WIDTH ** -0.5),
        'ffn_w_gate': nrm((DEPTH, D_MODEL, FFN_HIDDEN), D_MODEL ** -0.5),
        'ffn_w_up': nrm((DEPTH, D_MODEL, FFN_HIDDEN), D_MODEL ** -0.5),
        'ffn_w_down': nrm((DEPTH, FFN_HIDDEN, D_MODEL), FFN_HIDDEN ** -0.5),
    }


def reference(x, meta_tokens, norm_mix_g, norm_ffn_g, final_norm_g,
              ev_w_in, ev_conv_w, ev_conv_b, ev_lru_wa, ev_lru_ba, ev_lru_wx, ev_lru_bx, ev_lru_lambda,
              ev_lq1, ev_lk1, ev_lq2, ev_lk2, ev_subln_g, ev_w_out,
              od_w_in, od_s5_lam_re, od_s5_lam_im, od_s5_log_dt, od_s5_b_re, od_s5_b_im,
              od_s5_c_re, od_s5_c_im, od_s5_d, od_glu_w, od_glu_b,
              od_rw_mu, od_rw_w0, od_rw_w2, od_rw_a0, od_rw_a2, od_rw_g2,
              od_rw_kk, od_rw_ka, od_rw_rk, od_rw_ln_w, od_rw_ln_b, od_w_out,
              ffn_w_gate, ffn_w_up, ffn_w_down):
    Bb = x.shape[0]
    meta = jnp.broadcast_to(meta_tokens[None].astype(x.dtype), (Bb, N_META, D_MODEL))
    h = jnp.concatenate([meta, x], axis=1)
    for layer in range(DEPTH):
        j = layer // 2
        hn = rmsnorm(h, norm_mix_g[layer])
        if layer % 2 == 0:
            lambda_init = 0.8 - 0.6 * math.exp(-0.3 * layer)
            z = hn @ ev_w_in[j]
            xa, ga, q, k, v = split_cols(z, (LRU_WIDTH, LRU_WIDTH, DA_WIDTH, DA_WIDTH, DA_WIDTH))
            xa = causal_depthwise_conv(xa, ev_conv_w[j], ev_conv_b[j])
            ya = rg_lru(xa, ev_lru_wa[j], ev_lru_ba[j], ev_lru_wx[j], ev_lru_bx[j], ev_lru_lambda[j])
            ya = ya * jax.nn.gelu(ga.astype(jnp.float32))
            yb = diff_attention(q, k, v, ev_lq1[j], ev_lk1[j], ev_lq2[j], ev_lk2[j], ev_subln_g[j], lambda_init)
            mix = jnp.concatenate([ya, yb], axis=-1) @ ev_w_out[j]
        else:
            z = hn @ od_w_in[j]
            u, dcols = z[..., :S5_WIDTH], z[..., S5_WIDTH:]
            yc = s5_layer(u, od_s5_lam_re[j], od_s5_lam_im[j], od_s5_log_dt[j], od_s5_b_re[j], od_s5_b_im[j],
                          od_s5_c_re[j], od_s5_c_im[j], od_s5_d[j])
            yc = jax.nn.gelu(yc)
            yc = yc * jax.nn.sigmoid(yc @ od_glu_w[j].astype(jnp.float32) + od_glu_b[j].astype(jnp.float32))
            yd = rwkv7_time_mix(dcols, od_rw_mu[j], od_rw_w0[j], od_rw_w2[j], od_rw_a0[j], od_rw_a2[j],
                                od_rw_g2[j], od_rw_kk[j], od_rw_ka[j], od_rw_rk[j], od_rw_ln_w[j], od_rw_ln_b[j])
            mix = jnp.concatenate([yc, yd], axis=-1) @ od_w_out[j]
        h = h + mix.astype(h.dtype)
        h = h + swiglu(rmsnorm(h, norm_ffn_g[layer]), ffn_w_gate[layer], ffn_w_up[layer], ffn_w_down[layer]).astype(h.dtype)
    return rmsnorm(h, final_norm_g)[:, N_META:]
```

```python
import contextlib
import numpy as np
import concourse.bass as bass
import concourse.mybir as mybir
from concourse.bass_utils import run_bass_kernel_spmd

F32 = mybir.dt.float32
BF16 = mybir.dt.bfloat16
I32 = mybir.dt.int32
AF = mybir.ActivationFunctionType
ALU = mybir.AluOpType
AX = mybir.AxisListType


class Tile:
    __slots__ = ("name", "t", "w", "r", "excl")

    def __init__(self, name, t=None):
        self.name = name
        self.t = t
        self.w = None
        self.r = []
        self.excl = False

    def __getitem__(self, idx):
        return self.t[idx]


class View:
    __slots__ = ("name", "t", "parent")

    def __init__(self, name, t, parent):
        self.name = name
        self.t = t
        self.parent = parent

    def __getitem__(self, idx):
        return self.t[idx]

    @property
    def excl(self):
        return self.parent.excl

    @property
    def w(self):
        return self.parent.w

    @w.setter
    def w(self, v):
        self.parent.w = v

    @property
    def r(self):
        return self.parent.r

    @r.setter
    def r(self, v):
        self.parent.r = v


class Prog:
    ENG = ("pe", "dve", "act", "pool", "sp")
    NSLOT = 8

    def __init__(self, name="k"):
        self.nc = bass.Bass("TRN2", target_bir_lowering=False)
        self.es = contextlib.ExitStack()
        self.streams = {e: [] for e in self.ENG}
        self.cnt = {e: 0 for e in self.ENG}
        self.sem = {}
        for e in ("pe", "dve", "act", "pool"):
            self.sem[e] = self.es.enter_context(self.nc.semaphore("s_" + e))
        self.dq = {}
        for q in ("sp", "pool", "act"):
            self.dq[q] = {"n": 0, "sems": [self.es.enter_context(self.nc.semaphore(f"d_{q}{i}")) for i in range(self.NSLOT)]}
        self.seen = {e: {} for e in self.ENG}
        self.ntile = 0
        self.same_engine_sync = True

    def dram(self, name, shape, dt, kind):
        t = self.nc.dram_tensor(name, list(shape), dt, kind=kind)
        return Tile(name, t.ap())

    def sbuf(self, name, shape, dt):
        t = self.es.enter_context(self.nc.sbuf_tensor(name, list(shape), dt))
        return Tile(name, t)

    def psum(self, name, shape, dt=F32):
        t = self.es.enter_context(self.nc.psum_tensor(name, list(shape), dt))
        tl = Tile(name, t)
        tl.excl = True
        return tl

    def view(self, name, t, parent=None):
        if parent is not None:
            return View(name, t, parent)
        return Tile(name, t)

    def _waits(self, eng, reads, writes, pe_sync=False):
        need = {}
        def add(ev):
            if ev is None:
                return
            key, sem, val, src = ev
            if src == eng and eng == "pe" and not pe_sync:
                return
            if src == eng and not self.same_engine_sync and key in ("pe", "dve", "act", "pool"):
                return
            if self.seen[eng].get(key, 0) >= val:
                return
            if key not in need or need[key][1] < val:
                need[key] = (sem, val)
        for t in reads:
            add(t.w)
        for t in writes:
            add(t.w)
            for ev in t.r:
                add(ev)
        out = []
        for key, (sem, val) in need.items():
            self.seen[eng][key] = val
            out.append((sem, val))
        return out

    def I(self, eng, method, *args, reads=(), writes=(), pe_sync=False, **kwargs):
        xr = [t for t in reads if t.excl]
        if xr:
            reads = [t for t in reads if not t.excl]
            writes = list(writes) + xr
        waits = self._waits(eng, reads, writes, pe_sync=pe_sync)
        self.cnt[eng] += 1
        val = self.cnt[eng]
        sem = self.sem[eng]
        ev = (eng, sem, val, eng)
        for t in reads:
            t.r.append(ev)
        for t in writes:
            t.w = ev
            t.r = []
        self.streams[eng].append((waits, method, args, kwargs, sem, 1))

    def dma(self, q, out_ap, in_ap, reads=(), writes=(), **kw):
        d = self.dq[q]
        k = d["n"]
        d["n"] += 1
        slot = k % self.NSLOT
        sem = d["sems"][slot]
        val = 16 * (k // self.NSLOT + 1)
        key = f"d_{q}{slot}"
        waits = self._waits(q, reads, writes)
        if k >= self.NSLOT:
            pv = val - 16
            if self.seen[q].get(key, 0) < pv:
                self.seen[q][key] = pv
                waits.append((sem, pv))
        ev = (key, sem, val, "dma_" + q)
        for t in reads:
            t.r.append(ev)
        for t in writes:
            t.w = ev
            t.r = []
        kw = dict(kw); kw["out"] = out_ap; kw["in_"] = in_ap
        if q == "pool" and "max_dma_last_dim" not in kw:
            kw["max_dma_last_dim"] = 4096
        self.streams[q].append((waits, "dma_start", (), kw, sem, 16))
        return ev

    def finish(self, final_tiles):
        waits = self._waits("sp", final_tiles, [])
        self.streams["sp"].append((waits, None, (), {}, None, 0))
        nc = self.nc
        def run(e, items):
            for (waits, method, args, kwargs, sem, inc) in items:
                for (s, v) in waits:
                    e.wait_ge(s, v)
                if method is not None:
                    getattr(e, method)(*args, **kwargs).then_inc(sem, inc)
        with nc.Block() as block:
            @block.sync
            def _(e):
                run(e, self.streams["sp"])
            @block.tensor
            def _(e):
                run(e, self.streams["pe"])
            @block.vector
            def _(e):
                run(e, self.streams["dve"])
            @block.scalar
            def _(e):
                run(e, self.streams["act"])
            @block.gpsimd
            def _(e):
                run(e, self.streams["pool"])
        self.es.close()
        return nc


D = 2048
KT = 16
FF = 5632
FT = 44
NTOK = 2064
SUPER = [(0, 528), (528, 512), (1040, 512), (1552, 512)]
EPS = 1e-6


def subblocks(n):
    out, o = [], 0
    while o < n:
        w = min(512, n - o)
        out.append((o, w))
        o += w
    return out


def build_lin(mode_mix, ncols_next, final, glu, ntok=NTOK, super_blocks=SUPER, ff=FF):
    P = Prog()
    ft = ff // 128
    TBMAX = max(s for _, s in super_blocks)
    hT = P.dram("hT", [D, ntok], F32, "ExternalInput")
    gvec = P.dram("gvec", [128, 2 * KT], F32, "ExternalInput")
    outs = []
    if mode_mix:
        mixT = P.dram("mixT", [D, ntok], F32, "ExternalInput")
        w_out = P.dram("w_out", [D, D], F32, "ExternalInput")
        w_gate = P.dram("w_gate", [D, ff], F32, "ExternalInput")
        w_up = P.dram("w_up", [D, ff], F32, "ExternalInput")
        w_down = P.dram("w_down", [ff, D], F32, "ExternalInput")
        if glu:
            glu_w = P.dram("glu_w", [1024, 1024], F32, "ExternalInput")
            glu_b = P.dram("glu_b", [128, 8], F32, "ExternalInput")
        if not final:
            hT_out = P.dram("hT_out", [D, ntok], F32, "ExternalOutput")
            outs.append(hT_out)
    if ncols_next:
        w_in = P.dram("w_in", [D, ncols_next], F32, "ExternalInput")
        zT = P.dram("zT", [ncols_next, ntok], F32, "ExternalOutput")
        outs.append(zT)
    if final:
        outT = P.dram("outT", [D, ntok], F32, "ExternalOutput")
        outs.append(outT)

    hb = [P.sbuf(f"hb{k}", [128, TBMAX], F32) for k in range(KT)]
    xb = [P.sbuf(f"xb{k}", [128, TBMAX], BF16) for k in range(KT)]
    G = P.sbuf("G", [128, 2 * KT], F32)
    ones = P.sbuf("ones", [128, 128], BF16)
    sq = [P.sbuf(f"sq{i}", [128, 512], BF16) for i in range(2)]
    rstd = P.sbuf("rstd", [128, TBMAX], F32)
    epsb = P.sbuf("epsb", [128, 1], F32)
    WB = [P.sbuf(f"WB{i}", [128, max(ft, 2 * KT) * 256], BF16) for i in range(2)]
    WBg = [P.view(f"WBg{i}", WB[i].t[:, 0:KT * 256].rearrange("p (k c) -> p k c", c=256)) for i in range(2)]
    WBu = [P.view(f"WBu{i}", WB[i].t[:, KT * 256:2 * KT * 256].rearrange("p (k c) -> p k c", c=256)) for i in range(2)]
    WBd = [P.view(f"WBd{i}", WB[i].t[:, 0:ft * 256].rearrange("p (k c) -> p k c", c=256)) for i in range(2)]
    ostage = [P.sbuf(f"ost{i}", [128, 512], F32) for i in range(2)]
    if mode_mix:
        act = [P.sbuf(f"act{f}", [128, TBMAX], BF16) for f in range(ft)]
        sg = [P.sbuf(f"sg{i}", [128, 512], F32) for i in range(2)]
        if glu:
            gb = [P.sbuf(f"gb{k}", [128, TBMAX], BF16) for k in range(8)]
            GB = P.sbuf("GB", [128, 8], F32)
    pA = [P.psum(f"pA{i}", [128, 512]) for i in range(3)]
    pB = [P.psum(f"pB{i}", [128, 512]) for i in range(3)]
    pS = [P.psum(f"pS{i}", [128, 512]) for i in range(2)]
    cnt = {}

    def nxt(key, lst):
        i = cnt.get(key, 0)
        cnt[key] = i + 1
        return lst[i % len(lst)]

    P.dma("sp", G[:], gvec[:], writes=[G])
    P.I("pool", "memset", ones[:], 1.0, writes=[ones])
    P.I("pool", "memset", epsb[:], EPS, writes=[epsb])
    if mode_mix and glu:
        P.dma("sp", GB[:], glu_b[:], writes=[GB])

    def rmsnorm(src, gcol0, dst, n_tb):
        for (off, n) in subblocks(n_tb):
            ps = nxt("S", pS)
            for k in range(KT):
                s = nxt("sq", sq)
                P.I("act", "activation", s[:, :n], src[k][:, off:off + n], AF.Square, reads=[src[k]], writes=[s])
                P.I("pe", "matmul", ps[:, :n], ones[:], s[:, :n], start=(k == 0), stop=(k == KT - 1), reads=[s, ones], writes=[ps])
            P.I("act", "activation", rstd[:, off:off + n], ps[:, :n], AF.Sqrt, bias=epsb[:, 0:1], scale=1.0 / D, reads=[ps, epsb], writes=[rstd])
            P.I("dve", "reciprocal", rstd[:, off:off + n], rstd[:, off:off + n], reads=[rstd], writes=[rstd])
            for k in range(KT):
                P.I("dve", "scalar_tensor_tensor", dst[k][:, off:off + n], src[k][:, off:off + n], G[:, gcol0 + k:gcol0 + k + 1], rstd[:, off:off + n], ALU.mult, ALU.mult,
                    reads=[src[k], G, rstd], writes=[dst[k]])

    def load_w_cols(W_, c0, cw, nk, buf, extra=()):
        P.dma("pool", buf[:, 0:nk, 0:cw], W_.t.rearrange("(k p) c -> p k c", p=128)[:, 0:nk, c0:c0 + cw], writes=[buf] + list(extra))

    def mm_acc(ps, w, n, lhs_list, rhs_list, reads_l, reads_r):
        K = len(lhs_list)
        for k in range(K):
            P.I("pe", "matmul", ps[:w, :n], lhs_list[k], rhs_list[k], start=(k == 0), stop=(k == K - 1), reads=[reads_l, reads_r[k]], writes=[ps])

    for (t0, tb) in super_blocks:
        subs = subblocks(tb)
        for k in range(KT):
            P.dma("sp", hb[k][:, :tb], hT[k * 128:(k + 1) * 128, t0:t0 + tb], writes=[hb[k]])
        if mode_mix:
            for k in range(KT):
                P.dma("pool", xb[k][:, :tb], mixT[k * 128:(k + 1) * 128, t0:t0 + tb], writes=[xb[k]])
            src_mix = list(xb)
            if glu:
                for cp in range(4):
                    i = nxt("wb", [0, 1])
                    load_w_cols(glu_w, cp * 256, 256, 8, WBg[i], extra=[WBd[i]])
                    for cc in range(2):
                        c = cp * 2 + cc
                        for (off, n) in subs:
                            ps = nxt("A", pA)
                            mm_acc(ps, 128, n, [WBg[i][:, k, cc * 128:(cc + 1) * 128] for k in range(8)], [xb[k][:, off:off + n] for k in range(8)], WBg[i], xb)
                            s = nxt("sg", sg)
                            P.I("act", "activation", s[:, :n], ps[:, :n], AF.Sigmoid, bias=GB[:, c:c + 1], scale=1.0, reads=[ps, GB], writes=[s])
                            P.I("dve", "tensor_tensor", gb[c][:, off:off + n], xb[c][:, off:off + n], s[:, :n], ALU.mult, reads=[s, xb[c]], writes=[gb[c]])
                for c in range(8):
                    src_mix[c] = gb[c]
            for cp in range(8):
                i = nxt("wb", [0, 1])
                load_w_cols(w_out, cp * 256, 256, KT, WBg[i], extra=[WBd[i]])
                for cc in range(2):
                    c = cp * 2 + cc
                    for (off, n) in subs:
                        ps = nxt("A", pA)
                        mm_acc(ps, 128, n, [WBg[i][:, k, cc * 128:(cc + 1) * 128] for k in range(KT)], [src_mix[k][:, off:off + n] for k in range(KT)], WBg[i], src_mix)
                        P.I("dve", "tensor_tensor", hb[c][:, off:off + n], hb[c][:, off:off + n], ps[:, :n], ALU.add, reads=[ps, hb[c]], writes=[hb[c]])
            rmsnorm(hb, 0, xb, tb)
            for fp in range(ft // 2):
                i = nxt("wb", [0, 1])
                load_w_cols(w_gate, fp * 256, 256, KT, WBg[i], extra=[WBd[i]])
                load_w_cols(w_up, fp * 256, 256, KT, WBu[i], extra=[WBd[i]])
                for fc in range(2):
                    f = fp * 2 + fc
                    for (off, n) in subs:
                        pg = nxt("A", pA)
                        pu = nxt("B", pB)
                        mm_acc(pg, 128, n, [WBg[i][:, k, fc * 128:(fc + 1) * 128] for k in range(KT)], [xb[k][:, off:off + n] for k in range(KT)], WBg[i], xb)
                        mm_acc(pu, 128, n, [WBu[i][:, k, fc * 128:(fc + 1) * 128] for k in range(KT)], [xb[k][:, off:off + n] for k in range(KT)], WBu[i], xb)
                        s = nxt("sg", sg)
                        P.I("act", "activation", s[:, :n], pg[:, :n], AF.Silu, reads=[pg], writes=[s])
                        P.I("dve", "tensor_tensor", act[f][:, off:off + n], s[:, :n], pu[:, :n], ALU.mult, reads=[s, pu], writes=[act[f]])
            for dp in range(8):
                i = nxt("wb", [0, 1])
                P.dma("pool", WBd[i][:, :, :], w_down.t.rearrange("(k p) c -> p k c", p=128)[:, :, dp * 256:(dp + 1) * 256], writes=[WBd[i], WBg[i], WBu[i]])
                for dc in range(2):
                    d = dp * 2 + dc
                    for (off, n) in subs:
                        ps = nxt("A", pA)
                        mm_acc(ps, 128, n, [WBd[i][:, f, dc * 128:(dc + 1) * 128] for f in range(ft)], [act[f][:, off:off + n] for f in range(ft)], WBd[i], act)
                        P.I("dve", "tensor_tensor", hb[d][:, off:off + n], hb[d][:, off:off + n], ps[:, :n], ALU.add, reads=[ps, hb[d]], writes=[hb[d]])
            if not final:
                for k in range(KT):
                    P.dma("sp", hT_out[k * 128:(k + 1) * 128, t0:t0 + tb], hb[k][:, :tb], reads=[hb[k]], writes=[hT_out])
        if final:
            rmsnorm(hb, KT, hb, tb)
            for k in range(KT):
                P.dma("sp", outT[k * 128:(k + 1) * 128, t0:t0 + tb], hb[k][:, :tb], reads=[hb[k]], writes=[outT])
        if ncols_next:
            rmsnorm(hb, KT, xb, tb)
            c0 = 0
            while c0 < ncols_next:
                cw = min(256, ncols_next - c0)
                i = nxt("wb", [0, 1])
                load_w_cols(w_in, c0, cw, KT, WBg[i], extra=[WBd[i]])
                cc0 = 0
                while cc0 < cw:
                    w = min(128, cw - cc0)
                    for (off, n) in subs:
                        ps = nxt("A", pA)
                        mm_acc(ps, w, n, [WBg[i][:, k, cc0:cc0 + w] for k in range(KT)], [xb[k][:, off:off + n] for k in range(KT)], WBg[i], xb)
                        o = nxt("os", ostage)
                        P.I("act", "copy", o[:w, :n], ps[:w, :n], reads=[ps], writes=[o])
                        P.dma("sp", zT[c0 + cc0:c0 + cc0 + w, t0 + off:t0 + off + n], o[:w, :n], reads=[o], writes=[zT])
                    cc0 += w
                c0 += cw
    return P.finish(outs)


NMETA = 16


def build_even(T, tc_lru=1026):
    P = Prog()
    NQ = (T - NMETA) // 256
    assert NMETA + 256 * NQ == T
    NKB = 1 + 2 * NQ
    xaT = P.dram("xaT", [256, T], F32, "ExternalInput")
    gaT = P.dram("gaT", [256, T], F32, "ExternalInput")
    lruvec = P.dram("lruvec", [128, 2, 8], F32, "ExternalInput")
    wa = P.dram("wa", [2, 128, 128], F32, "ExternalInput")
    wx = P.dram("wx", [2, 128, 128], F32, "ExternalInput")
    qT = P.dram("qT", [2, 128, T], F32, "ExternalInput")
    kT = P.dram("kT", [2, 128, T], F32, "ExternalInput")
    vv = P.dram("vv", [2, T, 128], F32, "ExternalInput")
    lqk = P.dram("lqk", [1, 256], F32, "ExternalInput")
    sublng = P.dram("sublng", [1, 128], F32, "ExternalInput")
    cst = P.dram("cst", [1, 8], F32, "ExternalInput")
    NB = 2 + NQ + 2 * NQ + 2
    dtab = P.dram("dtab", [1, NB], F32, "ExternalInput")
    yaT = P.dram("yaT", [256, T], F32, "ExternalOutput")
    yb = P.dram("yb", [2, T, 128], F32, "ExternalOutput")

    LV = P.sbuf("LV", [128, 2, 8], F32)
    P.dma("sp", LV[:], lruvec[:], writes=[LV])
    CST = P.sbuf("CST", [128, 8], F32)
    P.dma("sp", CST[:], cst.t.partition_broadcast(128), writes=[CST])
    DT = P.sbuf("DT", [128, NB], F32)
    P.dma("sp", DT[:], dtab.t.partition_broadcast(128), writes=[DT])
    LQ = P.sbuf("LQ", [128, 256], F32)
    P.dma("sp", LQ[:], lqk.t.partition_broadcast(128), writes=[LQ])
    SG = P.sbuf("SG", [128, 128], F32)
    P.dma("sp", SG[:], sublng.t.partition_broadcast(128), writes=[SG])
    one_c = P.sbuf("one_c", [128, 1], F32)
    P.I("pool", "memset", one_c[:], 1.0, writes=[one_c])
    eps5 = P.sbuf("eps5", [128, 1], F32)
    P.I("pool", "memset", eps5[:], 1e-5, writes=[eps5])
    onesb = P.sbuf("onesb", [128, 1], BF16)
    P.I("pool", "memset", onesb[:], 1.0, writes=[onesb])
    identb = P.sbuf("identb", [128, 128], BF16)
    P.I("pool", "memset", identb[:], 0.0, writes=[identb])
    P.I("pool", "affine_select", identb[:], identb[:], pattern=[[-1, 128]], compare_op=ALU.not_equal, fill=1.0, base=0, channel_multiplier=1, reads=[identb], writes=[identb])
    MASK = P.sbuf("MASK", [128, 2, 256], BF16)
    P.I("pool", "memset", MASK[:], 0.0, writes=[MASK])
    for cfg in range(2):
        P.I("pool", "affine_select", MASK[:, cfg, :], MASK[:, cfg, :], pattern=[[1, 256]], compare_op=ALU.is_ge, fill=-30000.0, base=-128 * cfg, channel_multiplier=-1, reads=[MASK], writes=[MASK])

    pM = [P.psum(f"pM{i}", [128, 512]) for i in range(2)]
    pS = [P.psum(f"pS{i}", [128, 2, 256]) for i in range(2)]
    pAcc = [[P.psum(f"pAcc{m}{s}", [128, 512]) for s in range(2)] for m in range(2)]
    cnt = {}

    def nxt(key, lst):
        i = cnt.get(key, 0)
        cnt[key] = i + 1
        return lst[i % len(lst)]

    TC = tc_lru
    chunks = []
    o = 0
    while o < T:
        n = min(TC, T - o)
        chunks.append((o, n))
        o += n
    WA = P.sbuf("WA", [128, 2, 128], F32)
    WX = P.sbuf("WX", [128, 2, 128], F32)
    P.dma("sp", WA[:], wa.t.rearrange("n c d -> c n d"), writes=[WA])
    P.dma("sp", WX[:], wx.t.rearrange("n c d -> c n d"), writes=[WX])
    cvec = P.sbuf("cvec", [128, 2], F32)
    tmpc = P.sbuf("tmpc", [128, 2], F32)
    P.I("act", "activation", tmpc[:], LV[:, :, 7], AF.Exp, scale=-1.0, reads=[LV], writes=[tmpc])
    P.I("act", "activation", tmpc[:], tmpc[:], AF.Ln, bias=one_c[:, 0:1], scale=1.0, reads=[tmpc, one_c], writes=[tmpc])
    P.I("dve", "tensor_scalar", cvec[:], tmpc[:], -8.0, None, ALU.mult, reads=[tmpc], writes=[cvec])
    XA = [P.sbuf(f"XA{i}", [128, 3 + TC], F32) for i in range(2)]
    GA = [P.sbuf(f"GA{i}", [128, TC], F32) for i in range(2)]
    CV = P.sbuf("CV", [128, TC], F32)
    R = P.sbuf("R", [128, TC], F32)
    II = P.sbuf("II", [128, TC], F32)
    A = P.sbuf("A", [128, TC], F32)
    B = P.sbuf("B", [128, TC], F32)
    H = [P.sbuf(f"H{i}", [128, TC], F32) for i in range(2)]
    GL = P.sbuf("GL", [128, TC], F32)
    GW = P.sbuf("GW", [128, TC], F32)
    YA = [P.sbuf(f"YA{i}", [128, TC], F32) for i in range(2)]
    for c in range(2):
        prevH = None
        for ci, (t0, tn) in enumerate(chunks):
            xa = nxt("xa", XA)
            ga = nxt("ga", GA)
            if t0 == 0:
                P.I("pool", "memset", xa[:, 0:3], 0.0, writes=[xa])
                P.dma("sp", xa[:, 3:3 + tn], xaT[c * 128:(c + 1) * 128, 0:tn], writes=[xa])
            else:
                P.dma("sp", xa[:, 0:3 + tn], xaT[c * 128:(c + 1) * 128, t0 - 3:t0 + tn], writes=[xa])
            P.dma("sp", ga[:, 0:tn], gaT[c * 128:(c + 1) * 128, t0:t0 + tn], writes=[ga])
            P.I("dve", "tensor_scalar", CV[:, :tn], xa[:, 3:3 + tn], LV[:, c, 3:4], LV[:, c, 4:5], ALU.mult, ALU.add, reads=[xa, LV], writes=[CV])
            for j in range(3):
                P.I("dve", "scalar_tensor_tensor", CV[:, :tn], xa[:, j:j + tn], LV[:, c, j:j + 1], CV[:, :tn], ALU.mult, ALU.add, reads=[xa, LV, CV], writes=[CV])
            o2 = 0
            while o2 < tn:
                n = min(512, tn - o2)
                ps = nxt("M", pM)
                P.I("pe", "matmul", ps[:, :n], WA[:, c, :], CV[:, o2:o2 + n], start=True, stop=True, reads=[WA, CV], writes=[ps])
                P.I("act", "activation", R[:, o2:o2 + n], ps[:, :n], AF.Sigmoid, bias=LV[:, c, 5:6], scale=1.0, reads=[ps, LV], writes=[R])
                ps = nxt("M", pM)
                P.I("pe", "matmul", ps[:, :n], WX[:, c, :], CV[:, o2:o2 + n], start=True, stop=True, reads=[WX, CV], writes=[ps])
                P.I("act", "activation", II[:, o2:o2 + n], ps[:, :n], AF.Sigmoid, bias=LV[:, c, 6:7], scale=1.0, reads=[ps, LV], writes=[II])
                o2 += n
            P.I("act", "activation", A[:, :tn], R[:, :tn], AF.Exp, scale=cvec[:, c:c + 1], reads=[R, cvec], writes=[A])
            P.I("pool", "tensor_tensor", B[:, :tn], A[:, :tn], A[:, :tn], ALU.mult, reads=[A], writes=[B])
            P.I("act", "activation", B[:, :tn], B[:, :tn], AF.Sqrt, bias=one_c[:, 0:1], scale=-1.0, reads=[B, one_c], writes=[B])
            P.I("pool", "tensor_tensor", II[:, :tn], II[:, :tn], CV[:, :tn], ALU.mult, reads=[II, CV], writes=[II])
            P.I("dve", "tensor_tensor", B[:, :tn], B[:, :tn], II[:, :tn], ALU.mult, reads=[B, II], writes=[B])
            h = nxt("h", H)
            init = 0.0 if prevH is None else prevH[0][:, prevH[1] - 1:prevH[1]]
            P.I("dve", "tensor_tensor_scan", h[:, :tn], A[:, :tn], B[:, :tn], init, ALU.mult, ALU.add, reads=[A, B] + ([prevH[0]] if prevH else []), writes=[h])
            prevH = (h, tn)
            P.I("act", "activation", GW[:, :tn], ga[:, :tn], AF.Square, reads=[ga], writes=[GW])
            P.I("pool", "tensor_scalar", GW[:, :tn], GW[:, :tn], 0.044715, 1.0, ALU.mult, ALU.add, reads=[GW], writes=[GW])
            P.I("pool", "tensor_tensor", GW[:, :tn], GW[:, :tn], ga[:, :tn], ALU.mult, reads=[GW, ga], writes=[GW])
            P.I("act", "activation", GL[:, :tn], GW[:, :tn], AF.Sigmoid, scale=1.5957691216057308, reads=[GW], writes=[GL])
            P.I("pool", "tensor_tensor", GL[:, :tn], GL[:, :tn], ga[:, :tn], ALU.mult, reads=[GL, ga], writes=[GL])
            ya = nxt("ya", YA)
            P.I("dve", "tensor_tensor", ya[:, :tn], h[:, :tn], GL[:, :tn], ALU.mult, reads=[h, GL], writes=[ya])
            P.dma("sp", yaT[c * 128:(c + 1) * 128, t0:t0 + tn], ya[:, :tn], reads=[ya], writes=[yaT])

    pr = P.sbuf("pr", [128, 64], F32)
    e12 = P.sbuf("e12", [128, 2], F32)
    neglam = P.sbuf("neglam", [128, 1], F32)
    for i in range(2):
        P.I("dve", "tensor_tensor", pr[:], LQ[:, 128 * i:128 * i + 64], LQ[:, 128 * i + 64:128 * i + 128], ALU.mult, reads=[LQ], writes=[pr])
        P.I("dve", "tensor_reduce", e12[:, i:i + 1], pr[:], AX.X, ALU.add, reads=[pr], writes=[e12])
    P.I("act", "activation", e12[:], e12[:], AF.Exp, reads=[e12], writes=[e12])
    P.I("dve", "tensor_tensor", neglam[:], e12[:, 1:2], e12[:, 0:1], ALU.subtract, reads=[e12], writes=[neglam])
    P.I("dve", "tensor_tensor", neglam[:], neglam[:], CST[:, 2:3], ALU.subtract, reads=[neglam, CST], writes=[neglam])
    G2 = P.sbuf("G2", [128, 128], F32)
    P.I("dve", "tensor_scalar", G2[:], SG[:], CST[:, 3:4], None, ALU.mult, reads=[SG, CST], writes=[G2])

    NKR = 2 * NQ
    posk = P.sbuf("posk", [NKR, 128], F32)
    posq = P.sbuf("posq", [NQ, 256], F32)
    posm = P.sbuf("posm", [1, NMETA], F32)
    P.I("pool", "iota", posk[:], [[1, 128]], base=0, channel_multiplier=0, allow_small_or_imprecise_dtypes=True, writes=[posk])
    P.I("pool", "iota", posq[:], [[1, 256]], base=0, channel_multiplier=0, allow_small_or_imprecise_dtypes=True, writes=[posq])
    P.I("pool", "iota", posm[:], [[1, NMETA]], base=0, channel_multiplier=0, allow_small_or_imprecise_dtypes=True, writes=[posm])
    onesK = P.sbuf("onesK", [NKR, 128], BF16)
    P.I("pool", "memset", onesK[:], 1.0, writes=[onesK])
    onesm = P.sbuf("onesm", [1, NMETA], BF16)
    P.I("pool", "memset", onesm[:], 1.0, writes=[onesm])
    ones1 = P.sbuf("ones1", [1, 128], F32)
    P.I("pool", "memset", ones1[:], 1.0, writes=[ones1])
    stK = P.sbuf("stK", [NKR, 128], BF16)
    stQ = P.sbuf("stQ", [NQ, 256], BF16)
    stM = [P.sbuf(f"stM{i}", [NKR, 128], BF16) for i in range(2)]
    stm = [P.sbuf(f"stm{i}", [1, NMETA], BF16) for i in range(4)]
    negMc = P.sbuf("negMc", [128, 2], F32)
    scr = [P.dram(f"scr{i}", [1, T - NMETA], BF16, "Internal") for i in range(5)]

    def row_via_dram(si, src_tile, blk, dsts):
        P.dma("sp", scr[si].t.rearrange("o (a b) -> (o a) b", b=blk), src_tile[:], reads=[src_tile], writes=[scr[si]])
        for (dt_, r) in dsts:
            P.dma("sp", dt_[r:r + 1, NMETA:T], scr[si][:], reads=[scr[si]], writes=[dt_])

    QA = [P.sbuf(f"QA{m}", [67, T], BF16) for m in range(2)]
    KA = [P.sbuf(f"KA{m}", [67, T], BF16) for m in range(2)]
    VE = P.sbuf("VE", [128, NKB, 129], BF16)
    BT = P.sbuf("BT", [128, NB], F32)
    sqb = [P.sbuf(f"sqb{i}", [64, 512], BF16) for i in range(2)]
    NCH = (T + 511) // 512
    mx = P.sbuf("mx", [1, 4, NCH], F32)
    mx1 = P.sbuf("mx1", [1, 4], F32)
    negM = P.sbuf("negM", [1, 2], F32)
    PT = [P.sbuf(f"PT{i}", [128, 2, 256], BF16) for i in range(3)]
    O0 = P.sbuf("O0", [128, 128], F32)
    OO = P.sbuf("OO", [128, 128], F32)
    OS = P.sbuf("OS", [128, 128], F32)
    YB = [P.sbuf(f"YB{i}", [128, 128], F32) for i in range(2)]
    rc = P.sbuf("rc", [128, 4], F32)

    QB = [(0, NMETA)] + [(NMETA + 256 * i, 256) for i in range(NQ)]
    KB = [(0, NMETA)] + [(NMETA + 128 * j, 128) for j in range(2 * NQ)]
    dvals = [0.0] + [float(NMETA + 256 * i) for i in range(NQ)] + [128.0 * d for d in range(-1, 2 * NQ)]
    dvals = dvals + [0.0] * (NB - len(dvals))
    didx = {}
    for i_, v_ in enumerate(dvals):
        didx.setdefault(v_, i_)

    for h in range(2):
        for m in range(2):
            P.dma("pool", QA[m][0:64, :], qT[h, 64 * m:64 * m + 64, :], writes=[QA[m]])
            P.dma("pool", KA[m][0:64, :], kT[h, 64 * m:64 * m + 64, :], writes=[KA[m]])
        P.dma("pool", VE[0:NMETA, 0, 0:128], vv[h, 0:NMETA, :], writes=[VE])
        P.dma("pool", VE[:, 1:NKB, 0:128], vv[h, NMETA:T, :].rearrange("(j p) e -> p j e", p=128), writes=[VE])
        P.I("pool", "memset", VE[:, :, 128:129], 1.0, writes=[VE])
        P.I("dve", "tensor_scalar", BT[:], DT[:], CST[:, h:h + 1], -1.0, ALU.mult, ALU.mult, reads=[DT, CST], writes=[BT])
        P.I("dve", "tensor_scalar", stK[:], posk[:], CST[0:NKR, h:h + 1], 8.0, ALU.mult, ALU.mult, reads=[posk, CST], writes=[stK])
        P.I("dve", "tensor_scalar", stQ[:], posq[:], CST[0:NQ, h:h + 1], -8.0, ALU.mult, ALU.mult, reads=[posq, CST], writes=[stQ])
        P.I("dve", "tensor_scalar", stm[0][:], posm[:], CST[0:1, h:h + 1], 8.0, ALU.mult, ALU.mult, reads=[posm, CST], writes=[stm[0]])
        P.I("dve", "tensor_scalar", stm[1][:], posm[:], CST[0:1, h:h + 1], -8.0, ALU.mult, ALU.mult, reads=[posm, CST], writes=[stm[1]])
        row_via_dram(0, stK, 128, [(KA[0], 64), (KA[1], 64)])
        row_via_dram(1, onesK, 128, [(KA[0], 65), (KA[1], 65), (KA[0], 66), (KA[1], 66), (QA[0], 64), (QA[1], 64)])
        row_via_dram(2, stQ, 256, [(QA[0], 65), (QA[1], 65)])
        for m in range(2):
            P.dma("sp", KA[m][64:65, 0:NMETA], stm[0][:], reads=[stm[0]], writes=[KA[m]])
            for r in (65, 66):
                P.dma("sp", KA[m][r:r + 1, 0:NMETA], onesm[:], reads=[onesm], writes=[KA[m]])
            P.dma("sp", QA[m][64:65, 0:NMETA], onesm[:], reads=[onesm], writes=[QA[m]])
            P.dma("sp", QA[m][65:66, 0:NMETA], stm[1][:], reads=[stm[1]], writes=[QA[m]])
        for m in range(2):
            for wi, src in enumerate((QA[m], KA[m])):
                for ch in range(NCH):
                    o2 = ch * 512
                    n = min(512, T - o2)
                    s = nxt("sqb", sqb)
                    P.I("act", "activation", s[:, :n], src[0:64, o2:o2 + n], AF.Square, reads=[src], writes=[s])
                    ps = nxt("M", pM)
                    P.I("pe", "matmul", ps[0:1, :n], onesb[0:64, 0:1], s[:, :n], start=True, stop=True, reads=[onesb, s], writes=[ps])
                    P.I("dve", "tensor_reduce", mx[0:1, 2 * m + wi, ch:ch + 1], ps[0:1, :n], AX.X, ALU.max, reads=[ps], writes=[mx])
        P.I("dve", "tensor_reduce", mx1[:], mx[:], AX.X, ALU.max, reads=[mx], writes=[mx1])
        for m in range(2):
            P.I("dve", "tensor_tensor", negM[0:1, m:m + 1], mx1[0:1, 2 * m:2 * m + 1], mx1[0:1, 2 * m + 1:2 * m + 2], ALU.mult, reads=[mx1], writes=[negM])
        P.I("act", "activation", negM[:], negM[:], AF.Sqrt, reads=[negM], writes=[negM])
        ps = nxt("M", pM)
        P.I("pe", "matmul", ps[:, 0:2], ones1[0:1, :], negM[0:1, :], start=True, stop=True, reads=[ones1, negM], writes=[ps])
        P.I("act", "copy", negMc[:], ps[:, 0:2], reads=[ps], writes=[negMc])
        for m in range(2):
            P.I("dve", "tensor_scalar", stM[m][:], onesK[:], negMc[0:NKR, m:m + 1], -1.0, ALU.mult, ALU.mult, reads=[onesK, negMc], writes=[stM[m]])
            P.I("dve", "tensor_scalar", stm[2 + m][:], onesm[:], negMc[0:1, m:m + 1], -1.0, ALU.mult, ALU.mult, reads=[onesm, negMc], writes=[stm[2 + m]])
            row_via_dram(3 + m, stM[m], 128, [(QA[m], 66)])
            P.dma("sp", QA[m][66:67, 0:NMETA], stm[2 + m][:], reads=[stm[2 + m]], writes=[QA[m]])

        for (qs, qn) in QB:
            kbs = [(kbi, ks, kn) for kbi, (ks, kn) in enumerate(KB) if ks < qs + qn]
            nsub = (qn + 127) // 128
            for ii, (kbi, ks, kn) in enumerate(kbs):
                masked = (ks + kn - 1) > qs
                cfg = 0 if ks <= qs else 1
                ps = nxt("S", pS)
                for m in range(2):
                    P.I("pe", "matmul", ps[:kn, m, :qn], KA[m][0:67, ks:ks + kn], QA[m][0:67, qs:qs + qn], start=True, stop=(not masked), reads=[KA[m], QA[m]], writes=[ps])
                    if masked:
                        P.I("pe", "matmul", ps[:kn, m, :qn], identb[0:kn, 0:kn], MASK[0:kn, cfg, 0:qn], start=False, stop=True, reads=[identb, MASK], writes=[ps])
                pt = nxt("PT", PT)
                bi = didx[float(qs - ks)]
                P.I("act", "activation", pt[:kn, :, :qn], ps[:kn, :, :qn], AF.Exp, bias=BT[:kn, bi:bi + 1], scale=0.125, reads=[ps, BT], writes=[pt])
                for m in range(2):
                    for sb in range(nsub):
                        qsn = min(128, qn - sb * 128)
                        P.I("pe", "matmul", pAcc[m][sb][:qsn, 0:129], pt[:kn, m, sb * 128:sb * 128 + qsn], VE[:kn, kbi, :], start=(ii == 0), stop=(ii == len(kbs) - 1), reads=[pt, VE], writes=[pAcc[m][sb]])
            for sb in range(nsub):
                qsn = min(128, qn - sb * 128)
                a0 = pAcc[0][sb]
                a1 = pAcc[1][sb]
                P.I("dve", "reciprocal", rc[:qsn, 0:1], a0[:qsn, 128:129], reads=[a0], writes=[rc])
                P.I("dve", "reciprocal", rc[:qsn, 1:2], a1[:qsn, 128:129], reads=[a1], writes=[rc])
                P.I("dve", "tensor_tensor", rc[:qsn, 1:2], rc[:qsn, 1:2], neglam[:qsn, 0:1], ALU.mult, reads=[rc, neglam], writes=[rc])
                P.I("dve", "tensor_scalar", O0[:qsn, :], a0[:qsn, 0:128], rc[:qsn, 0:1], None, ALU.mult, reads=[a0, rc], writes=[O0])
                P.I("dve", "scalar_tensor_tensor", OO[:qsn, :], a1[:qsn, 0:128], rc[:qsn, 1:2], O0[:qsn, :], ALU.mult, ALU.add, reads=[a1, rc, O0], writes=[OO])
                P.I("act", "activation", OS[:qsn, :], OO[:qsn, :], AF.Square, accum_out=rc[:qsn, 2:3], reads=[OO], writes=[OS, rc])
                P.I("act", "activation", rc[:qsn, 3:4], rc[:qsn, 2:3], AF.Sqrt, bias=eps5[:qsn, 0:1], scale=1.0 / 128, reads=[rc, eps5], writes=[rc])
                P.I("dve", "reciprocal", rc[:qsn, 3:4], rc[:qsn, 3:4], reads=[rc], writes=[rc])
                y = nxt("YB", YB)
                P.I("dve", "scalar_tensor_tensor", y[:qsn, :], OO[:qsn, :], rc[:qsn, 3:4], G2[:qsn, :], ALU.mult, ALU.mult, reads=[OO, rc, G2], writes=[y])
                P.dma("sp", yb[h, qs + sb * 128:qs + sb * 128 + qsn, :], y[:qsn, :], reads=[y], writes=[yb])
    return P.finish([yaT, yb])


NMETA = 16
CH = 64
NEG_EXP_HALF = -0.6065306597126334
GN_EPS = 64e-5


def build_rwkv(T, debug=0, meta=True):
    P = Prog()
    if meta:
        assert (T - NMETA) % 512 == 0
        SC = [(0, NMETA)] + [(NMETA + 512 * k, 512) for k in range((T - NMETA) // 512)]
    else:
        assert T % 512 == 0
        SC = [(512 * k, 512) for k in range(T // 512)]
    rT = P.dram("rT", [256, T + 1], F32, "ExternalInput")
    kT = P.dram("kT", [256, T + 1], F32, "ExternalInput")
    vT = P.dram("vT", [256, T + 1], F32, "ExternalInput")
    wlT = P.dram("wlT", [96, T + 1], F32, "ExternalInput")
    alT = P.dram("alT", [96, T + 1], F32, "ExternalInput")
    glT = P.dram("glT", [256, T + 1], F32, "ExternalInput")
    pv = P.dram("pv", [128, 2, 12], F32, "ExternalInput")
    pl = P.dram("pl", [128, 4], F32, "ExternalInput")
    w2 = P.dram("w2", [96, 256], F32, "ExternalInput")
    a2 = P.dram("a2", [96, 256], F32, "ExternalInput")
    g2 = P.dram("g2", [256, 256], F32, "ExternalInput")
    ydT = P.dram("ydT", [256, T], F32, "ExternalOutput")
    st_in = P.dram("st_in", [2, 128, 64], F32, "ExternalInput")
    st_out = P.dram("st_out", [2, 128, 64], F32, "ExternalOutput")

    cnt = {}

    def nxt(key, lst):
        i = cnt.get(key, 0)
        cnt[key] = i + 1
        return lst[i % len(lst)]

    PV = P.sbuf("PV", [128, 2, 12], F32)
    P.dma("sp", PV[:], pv[:], writes=[PV])
    PL = P.sbuf("PL", [128, 4], F32)
    P.dma("sp", PL[:], pl[:], writes=[PL])
    W2 = P.sbuf("W2", [96, 256], F32)
    A2 = P.sbuf("A2", [96, 256], F32)
    G2 = P.sbuf("G2", [128, 2, 256], F32)
    P.dma("sp", W2[:], w2[:], writes=[W2])
    P.dma("sp", A2[:], a2[:], writes=[A2])
    P.dma("sp", G2[:], g2.t.rearrange("(k p) c -> p k c", p=128), writes=[G2])
    OMK = P.sbuf("OMK", [128, 2], F32)
    P.I("dve", "tensor_scalar", OMK[:], PV[:, :, 6], -1.0, 1.0, ALU.mult, ALU.add, reads=[PV], writes=[OMK])
    IDENT = P.sbuf("IDENT", [128, 128], F32)
    P.I("pool", "memset", IDENT[:], 0.0, writes=[IDENT])
    P.I("pool", "affine_select", IDENT[:], IDENT[:], pattern=[[-1, 128]], compare_op=ALU.not_equal, fill=1.0, base=0, channel_multiplier=1, reads=[IDENT], writes=[IDENT])
    BLK1 = P.sbuf("BLK1", [128, 128], F32)
    BLKM = P.sbuf("BLKM", [128, 128], F32)
    P.I("pool", "memset", BLK1[:], 0.0, writes=[BLK1])
    P.I("pool", "memset", BLKM[:], 0.0, writes=[BLKM])
    for h in range(2):
        P.I("pool", "memset", BLK1[h * 64:(h + 1) * 64, h * 64:(h + 1) * 64], 1.0, writes=[BLK1])
        P.I("pool", "memset", BLKM[h * 64:(h + 1) * 64, h * 64:(h + 1) * 64], 1.0 / 64, writes=[BLKM])
    ID2 = P.sbuf("ID2", [128, 64], F32)
    MUs = P.sbuf("MUs", [128, 64], F32)
    MUi = P.sbuf("MUi", [128, 64], F32)
    MLs = P.sbuf("MLs", [128, 64], F32)
    for t_ in (MUs, MUi, MLs):
        P.I("pool", "memset", t_[:], 1.0, writes=[t_])
    for h in range(2):
        sl = slice(h * 64, (h + 1) * 64)
        P.I("pool", "tensor_copy", ID2[sl, :], IDENT[sl, h * 64:(h + 1) * 64], reads=[IDENT], writes=[ID2])
        P.I("pool", "affine_select", MUs[sl, :], MUs[sl, :], pattern=[[1, 64]], compare_op=ALU.is_gt, fill=0.0, base=0, channel_multiplier=-1, reads=[MUs], writes=[MUs])
        P.I("pool", "affine_select", MUi[sl, :], MUi[sl, :], pattern=[[1, 64]], compare_op=ALU.is_ge, fill=0.0, base=0, channel_multiplier=-1, reads=[MUi], writes=[MUi])
        P.I("pool", "affine_select", MLs[sl, :], MLs[sl, :], pattern=[[-1, 64]], compare_op=ALU.is_gt, fill=0.0, base=0, channel_multiplier=1, reads=[MLs], writes=[MLs])
    RESET = P.sbuf("RESET", [128, 512], F32)
    P.I("pool", "memset", RESET[:], 1.0, writes=[RESET])
    P.I("pool", "memset", RESET.t[:, :].rearrange("p (a b) -> p a b", b=64)[:, :, 0:1], 0.0, writes=[RESET])
    epsg = P.sbuf("epsg", [128, 1], F32)
    P.I("pool", "memset", epsg[:], GN_EPS, writes=[epsg])

    gb = [P.psum(f"gb{p}", [128, 512]) for p in range(2)]
    db = [P.psum(f"db{p}", [128, 512]) for p in range(2)]
    cb = [P.psum(f"cb{p}", [128, 512]) for p in range(2)]
    pp = [P.psum(f"pp{i}", [128, 512]) for i in range(2)]

    def v3(t, c0):
        return t.t[:, c0:c0 + 128].rearrange("p (x y) -> p x y", y=64)
    g1 = [P.view(f"g1_{p}", v3(gb[p], 0), parent=gb[p]) for p in range(2)]
    g2p = [P.view(f"g2_{p}", v3(gb[p], 128), parent=gb[p]) for p in range(2)]
    g3 = [P.view(f"g3_{p}", gb[p].t[:, 256:320], parent=gb[p]) for p in range(2)]
    trp = [P.view(f"tr_{p}", gb[p].t[:, 320:384], parent=gb[p]) for p in range(2)]
    q1 = [[P.view(f"q1_{p}{i}", v3(db[p], 128 * i), parent=db[p]) for i in range(2)] for p in range(2)]
    q2 = [[P.view(f"q2_{p}{i}", db[p].t[:, 256 + 64 * i:320 + 64 * i], parent=db[p]) for i in range(2)] for p in range(2)]
    xtp = [P.view(f"xt{p}", cb[p].t[:, 0:64], parent=cb[p]) for p in range(2)]
    utp = [P.view(f"ut{p}", cb[p].t[:, 64:128], parent=cb[p]) for p in range(2)]
    spp = [P.view(f"sp{p}", cb[p].t[:, 128:192], parent=cb[p]) for p in range(2)]
    ypp = [P.view(f"yp{p}", cb[p].t[:, 192:256], parent=cb[p]) for p in range(2)]

    def T512(name):
        return P.sbuf(name, [128, 512], F32)
    XIN = {nm: [P.sbuf(f"X{nm}{i}", [128, 513], F32) for i in range(2)] for nm in ("r0", "k0", "v0", "r1", "k1", "v1", "wl", "al", "g0", "g1")}
    DD = [T512(f"DD{i}") for i in range(2)]
    WL = T512("WL"); AL = T512("AL"); GL0 = T512("GL0"); GL1 = T512("GL1")
    Rr = [T512(f"Rr{p}") for p in range(2)]; Kk = [T512(f"Kk{p}") for p in range(2)]; Vv = [T512(f"Vv{p}") for p in range(2)]
    LOGW = [T512(f"LOGW{p}") for p in range(2)]; AA = [T512(f"AA{p}") for p in range(2)]; GT = [T512(f"GT{p}") for p in range(2)]
    KKn = [T512(f"KKn{p}") for p in range(2)]; KM = [T512(f"KM{p}") for p in range(2)]
    LWC = [T512(f"LWC{p}") for p in range(2)]; GAM = [T512(f"GAM{p}") for p in range(2)]; GINV = [T512(f"GINV{p}") for p in range(2)]
    GPRV = [T512(f"GPRV{p}") for p in range(2)]
    AR = [P.sbuf(f"AR{p}", [128, 8, 2, 64], F32) for p in range(2)]
    BF = [T512(f"BF{p}") for p in range(2)]; KF = [T512(f"KF{p}") for p in range(2)]
    RKB = [T512(f"RKB{p}") for p in range(2)]
    TA = [T512(f"TA{i}") for i in range(3)]
    YS = [T512(f"YS{p}") for p in range(2)]; YC = [T512(f"YC{p}") for p in range(2)]
    def T64(name):
        return P.sbuf(name, [128, 64], F32)
    PSb = [[[P.sbuf(f"PS{p}{s}{i}", [128, 2, 64], F32) for i in range(2)] for s in range(2)] for p in range(2)]
    PTb = [[[T64(f"PT{p}{s}{i}") for i in range(2)] for s in range(2)] for p in range(2)]
    MBR = [[T64(f"MBR{p}{s}") for s in range(2)] for p in range(2)]
    NKB = [[T64(f"NKB{p}{s}") for s in range(2)] for p in range(2)]
    MKR = [[T64(f"MKR{p}{s}") for s in range(2)] for p in range(2)]
    VTt = [[T64(f"VTt{p}{s}") for s in range(2)] for p in range(2)]
    BTt = [[T64(f"BTt{p}{s}") for s in range(2)] for p in range(2)]
    KTt = [[T64(f"KTt{p}{s}") for s in range(2)] for p in range(2)]
    XTs = [T64(f"XTs{p}") for p in range(2)]
    UTs = [T64(f"UTs{p}") for p in range(2)]
    ST = [[T64(f"ST{p}{i}") for i in range(2)] for p in range(2)]
    for p in range(2):
        P.dma("sp", ST[p][0][:], st_in[p], writes=[ST[p][0]])
    ZR = P.sbuf("ZR", [128, 512], F32)
    P.I("pool", "memset", ZR[:], 0.0, writes=[ZR])
    for bank in (gb[0], gb[1], db[0], db[1], cb[0], cb[1], pp[0], pp[1]):
        P.I("pe", "matmul", bank[:, :], ZR[:, 0:128], ZR[:, :], start=True, stop=True, reads=[ZR], writes=[bank])
    for grp in (PSb, PTb):
        for a_ in grp:
            for b_ in a_:
                for t_ in b_:
                    P.I("pool", "memset", t_[:], 0.0, writes=[t_])
    for grp in (MBR, NKB, MKR, VTt, BTt, KTt):
        for a_ in grp:
            for t_ in a_:
                P.I("pool", "memset", t_[:], 0.0, writes=[t_])
    for t_ in XTs + UTs + [ST[0][1], ST[1][1]]:
        P.I("pool", "memset", t_[:], 0.0, writes=[t_])
    Tfin = {}

    HS = [slice(0, 64), slice(64, 128)]

    def pre_stream(p, s, c, n):
        ar = AR[p]
        for h in range(2):
            hs = slice(h * 64, h * 64 + n)
            P.I("pe", "matmul", g1[p][hs, :, 0:n], BF[p][HS[h], c * 64:c * 64 + n], ar[HS[h], c, :, 0:n], pe_sync=True, start=True, stop=True, reads=[BF[p], ar], writes=[g1[p]])
            P.I("pe", "matmul", g2p[p][hs, :, 0:n], KF[p][HS[h], c * 64:c * 64 + n], ar[HS[h], c, :, 0:n], pe_sync=True, start=True, stop=True, reads=[KF[p], ar], writes=[g2p[p]])
            P.I("pe", "matmul", g3[p][hs, 0:n], ar[HS[h], c, 0, 0:n], BF[p][HS[h], c * 64:c * 64 + n], pe_sync=True, start=True, stop=True, reads=[BF[p], ar], writes=[g3[p]])
        ps0 = PSb[p][s][0]
        pt0 = PTb[p][s][0]
        evn = 6
        if evn > 0:
            P.I("dve", "tensor_tensor", ps0[:, 0, :], g1[p][:, 0, :], MUs[:], ALU.mult, reads=[g1[p], MUs], writes=[ps0])
        if evn > 1:
            P.I("pool", "tensor_copy", ps0[:, 1, :], ID2[:], reads=[ID2], writes=[ps0])
        if evn > 2:
            P.I("dve", "tensor_tensor", MBR[p][s][:], g1[p][:, 1, :], MUi[:], ALU.mult, reads=[g1[p], MUi], writes=[MBR[p][s]])
        if evn > 3:
            P.I("dve", "tensor_tensor", NKB[p][s][:], g2p[p][:, 0, :], MUs[:], ALU.mult, reads=[g2p[p], MUs], writes=[NKB[p][s]])
        if evn > 4:
            P.I("dve", "tensor_tensor", MKR[p][s][:], g2p[p][:, 1, :], MUi[:], ALU.mult, reads=[g2p[p], MUi], writes=[MKR[p][s]])
        if evn > 5:
            P.I("dve", "tensor_tensor", pt0[:], g3[p][:], MLs[:], ALU.mult, reads=[g3[p], MLs], writes=[pt0])
        yield
        if debug == 3:
            return
        for (src, dst) in ((Vv[p], VTt[p][s]), (BF[p], BTt[p][s]), (KF[p], KTt[p][s])):
            tp = trp[p]
            for h in range(2):
                hs = slice(h * 64, h * 64 + n)
                P.I("pe", "matmul", tp[hs, 0:64], src[HS[h], c * 64:c * 64 + n], IDENT[HS[h], h * 64:(h + 1) * 64], pe_sync=True, start=True, stop=True, reads=[src, IDENT], writes=[tp])
            P.I("act", "copy", dst[:], tp[:], reads=[tp], writes=[dst])
            yield
        if debug == 4:
            return
        nlev = 6
        cur = 0
        for lev in range(nlev):
            ps_c, pt_c = PSb[p][s][cur], PTb[p][s][cur]
            ps_n, pt_n = PSb[p][s][1 - cur], PTb[p][s][1 - cur]
            qa = q1[p][lev % 2]
            qb = q2[p][lev % 2]
            last = (lev == nlev - 1)
            for h in range(2):
                hs = slice(h * 64, h * 64 + n)
                if not last:
                    P.I("pe", "matmul", qa[hs, :, 0:n], pt_c[hs, 0:n], ps_c[hs, :, 0:n], pe_sync=True, start=True, stop=True, reads=[pt_c, ps_c], writes=[qa])
                    P.I("pe", "matmul", qb[hs, 0:n], ps_c[hs, 0, 0:n], pt_c[hs, 0:n], pe_sync=True, start=True, stop=True, reads=[pt_c, ps_c], writes=[qb])
                else:
                    P.I("pe", "matmul", qa[hs, 1, 0:n], pt_c[hs, 0:n], ps_c[hs, 1, 0:n], pe_sync=True, start=True, stop=True, reads=[pt_c, ps_c], writes=[qa])
            if not last:
                P.I("act", "copy", ps_n[:, 0, :], qa[:, 0, :], reads=[qa], writes=[ps_n])
                P.I("act", "copy", pt_n[:], qb[:], reads=[qb], writes=[pt_n])
            P.I("dve", "tensor_tensor", ps_n[:, 1, :], ps_c[:, 1, :], qa[:, 1, :], ALU.add, reads=[qa, ps_c], writes=[ps_n])
            cur = 1 - cur
            yield
        Tfin[(p, s)] = PSb[p][s][cur]

    def chain_stream(p, s, c, n, sti):
        ar = AR[p]
        st_c, st_n = ST[p][sti], ST[p][1 - sti]
        tt = Tfin[(p, s)]
        o = c * 64
        for h in range(2):
            hs = slice(h * 64, h * 64 + n)
            P.I("pe", "matmul", xtp[p][hs, :], ar[HS[h], c, 0, 0:n], st_c[HS[h], :], pe_sync=True, start=True, stop=False, reads=[ar, st_c], writes=[xtp[p]])
            P.I("pe", "matmul", xtp[p][hs, :], NKB[p][s][hs, 0:n], VTt[p][s][hs, :], start=False, stop=True, reads=[NKB[p][s], VTt[p][s]], writes=[xtp[p]])
        P.I("act", "copy", XTs[p][:], xtp[p][:], reads=[xtp[p]], writes=[XTs[p]])
        yield
        for h in range(2):
            hs = slice(h * 64, h * 64 + n)
            P.I("pe", "matmul", utp[p][hs, :], tt[hs, 1, 0:n], XTs[p][hs, :], pe_sync=True, start=True, stop=True, reads=[tt, XTs[p]], writes=[utp[p]])
        P.I("dve", "tensor_copy", UTs[p][:], utp[p][:], reads=[utp[p]], writes=[UTs[p]])
        yield
        for h in range(2):
            hs = slice(h * 64, h * 64 + n)
            P.I("pe", "matmul", ypp[p][HS[h], 0:n], st_c[HS[h], :], ar[HS[h], c, 1, 0:n], pe_sync=True, start=True, stop=False, reads=[st_c, ar], writes=[ypp[p]])
            P.I("pe", "matmul", ypp[p][HS[h], 0:n], UTs[p][hs, :], MBR[p][s][hs, 0:n], start=False, stop=False, reads=[UTs[p], MBR[p][s]], writes=[ypp[p]])
            P.I("pe", "matmul", ypp[p][HS[h], 0:n], VTt[p][s][hs, :], MKR[p][s][hs, 0:n], start=False, stop=True, reads=[VTt[p][s], MKR[p][s]], writes=[ypp[p]])
        for h in range(2):
            hs = slice(h * 64, h * 64 + n)
            P.I("pe", "matmul", spp[p][HS[h], :], IDENT[HS[h], h * 64:(h + 1) * 64], st_c[HS[h], :], pe_sync=True, start=True, stop=False, reads=[IDENT, st_c], writes=[spp[p]])
            P.I("pe", "matmul", spp[p][HS[h], :], BTt[p][s][hs, :], UTs[p][hs, :], start=False, stop=False, reads=[BTt[p][s], UTs[p]], writes=[spp[p]])
            P.I("pe", "matmul", spp[p][HS[h], :], KTt[p][s][hs, :], VTt[p][s][hs, :], start=False, stop=True, reads=[KTt[p][s], VTt[p][s]], writes=[spp[p]])
        P.I("act", "activation", st_n[:], spp[p][:], AF.Copy, scale=GAM[p][:, o + n - 1:o + n], reads=[spp[p], GAM[p]], writes=[st_n])
        P.I("dve", "tensor_copy", YS[p][:, o:o + n], ypp[p][:, 0:n], reads=[ypp[p]], writes=[YS[p]])
        yield

    def run_streams(streams):
        streams = list(streams)
        while streams:
            alive = []
            for g in streams:
                try:
                    next(g)
                    alive.append(g)
                except StopIteration:
                    pass
            streams = alive

    def shift(dst, src, mu_col, n, np_=128):
        d = nxt("DD", DD)
        P.I("pool", "tensor_tensor", d[:np_, :n], src[:np_, 0:n], src[:np_, 1:1 + n], ALU.subtract, reads=[src], writes=[d])
        P.I("dve", "scalar_tensor_tensor", dst[:np_, :n], d[:np_, :n], mu_col, src[:np_, 1:1 + n], ALU.mult, ALU.add, reads=[d, src, PV, PL], writes=[dst])

    sti = [0, 0]
    chunk_counter = 0
    for (t0, n_sc) in SC:
        nchunk = (n_sc + 63) // 64
        xin = {}
        for nm, src, r0, np_ in (("r0", rT, 0, 128), ("k0", kT, 0, 128), ("v0", vT, 0, 128), ("r1", rT, 128, 128), ("k1", kT, 128, 128), ("v1", vT, 128, 128),
                                 ("wl", wlT, 0, 96), ("al", alT, 0, 96), ("g0", glT, 0, 128), ("g1", glT, 128, 128)):
            x = nxt("X" + nm, XIN[nm])
            xin[nm] = x
            P.dma("sp", x[:np_, 0:1 + n_sc], src[r0:r0 + np_, t0:t0 + 1 + n_sc], writes=[x])
        n = n_sc
        shift(WL, xin["wl"], PL[0:96, 0:1], n, 96)
        shift(AL, xin["al"], PL[0:96, 1:2], n, 96)
        shift(GL0, xin["g0"], PL[:, 2:3], n)
        shift(GL1, xin["g1"], PL[:, 3:4], n)
        P.I("act", "activation", WL[0:96, :n], WL[0:96, :n], AF.Tanh, reads=[WL], writes=[WL])
        P.I("act", "activation", GL0[:, :n], GL0[:, :n], AF.Sigmoid, reads=[GL0], writes=[GL0])
        P.I("act", "activation", GL1[:, :n], GL1[:, :n], AF.Sigmoid, reads=[GL1], writes=[GL1])
        for p in range(2):
            cs = slice(p * 128, (p + 1) * 128)
            shift(Rr[p], xin[f"r{p}"], PV[:, p, 0:1], n)
            shift(Kk[p], xin[f"k{p}"], PV[:, p, 1:2], n)
            shift(Vv[p], xin[f"v{p}"], PV[:, p, 2:3], n)
            ps = nxt("pp", pp)
            P.I("pe", "matmul", ps[:, :n], W2[0:96, cs], WL[0:96, :n], start=True, stop=True, reads=[W2, WL], writes=[ps])
            P.I("act", "activation", LOGW[p][:, :n], ps[:, :n], AF.Sigmoid, bias=PV[:, p, 3:4], scale=1.0, reads=[ps, PV], writes=[LOGW[p]])
            P.I("pool", "tensor_scalar", LOGW[p][:, :n], LOGW[p][:, :n], NEG_EXP_HALF, None, ALU.mult, reads=[LOGW[p]], writes=[LOGW[p]])
            ps = nxt("pp", pp)
            P.I("pe", "matmul", ps[:, :n], A2[0:96, cs], AL[0:96, :n], start=True, stop=True, reads=[A2, AL], writes=[ps])
            P.I("act", "activation", AA[p][:, :n], ps[:, :n], AF.Sigmoid, bias=PV[:, p, 4:5], scale=1.0, reads=[ps, PV], writes=[AA[p]])
            ps = nxt("pp", pp)
            P.I("pe", "matmul", ps[:, :n], G2[:, 0, cs], GL0[:, :n], start=True, stop=False, reads=[G2, GL0], writes=[ps])
            P.I("pe", "matmul", ps[:, :n], G2[:, 1, cs], GL1[:, :n], start=False, stop=True, reads=[G2, GL1], writes=[ps])
            P.I("act", "copy", GT[p][:, :n], ps[:, :n], reads=[ps], writes=[GT[p]])
            ta, tb = TA[0], TA[1]
            P.I("pool", "tensor_scalar", KKn[p][:, :n], Kk[p][:, :n], PV[:, p, 5:6], None, ALU.mult, reads=[Kk[p], PV], writes=[KKn[p]])
            P.I("act", "activation", ta[:, :n], KKn[p][:, :n], AF.Square, reads=[KKn[p]], writes=[ta])
            ps = nxt("pp", pp)
            P.I("pe", "matmul", ps[:, :n], BLK1[:], ta[:, :n], start=True, stop=True, reads=[BLK1, ta], writes=[ps])
            P.I("act", "activation", tb[:, :n], ps[:, :n], AF.Sqrt, reads=[ps], writes=[tb])
            P.I("dve", "tensor_scalar", tb[:, :n], tb[:, :n], 1e-12, None, ALU.max, reads=[tb], writes=[tb])
            P.I("dve", "reciprocal", tb[:, :n], tb[:, :n], reads=[tb], writes=[tb])
            P.I("dve", "tensor_tensor", KKn[p][:, :n], KKn[p][:, :n], tb[:, :n], ALU.mult, reads=[KKn[p], tb], writes=[KKn[p]])
            P.I("dve", "tensor_scalar", ta[:, :n], AA[p][:, :n], PV[:, p, 6:7], OMK[:, p:p + 1], ALU.mult, ALU.add, reads=[AA[p], PV, OMK], writes=[ta])
            P.I("pool", "tensor_tensor", KM[p][:, :n], Kk[p][:, :n], ta[:, :n], ALU.mult, reads=[Kk[p], ta], writes=[KM[p]])
            P.I("dve", "tensor_tensor_scan", LWC[p][:, :n], RESET[:, :n], LOGW[p][:, :n], 0.0, ALU.mult, ALU.add, reads=[RESET, LOGW[p]], writes=[LWC[p]])
            P.I("act", "activation", GAM[p][:, :n], LWC[p][:, :n], AF.Exp, reads=[LWC[p]], writes=[GAM[p]])
            P.I("act", "activation", GINV[p][:, :n], LWC[p][:, :n], AF.Exp, scale=-1.0, reads=[LWC[p]], writes=[GINV[p]])
            P.I("pool", "tensor_tensor", ta[:, :n], LWC[p][:, :n], LOGW[p][:, :n], ALU.subtract, reads=[LWC[p], LOGW[p]], writes=[ta])
            P.I("act", "activation", GPRV[p][:, :n], ta[:, :n], AF.Exp, reads=[ta], writes=[GPRV[p]])
            arv = AR[p].t
            if n == 512:
                outA = arv[:, :, 0, :]
                outR = arv[:, :, 1, :]
                inG = GPRV[p].t[:, :].rearrange("p (c t) -> p c t", t=64)
                inK = KKn[p].t[:, :].rearrange("p (c t) -> p c t", t=64)
                inR = Rr[p].t[:, :].rearrange("p (c t) -> p c t", t=64)
                inGm = GAM[p].t[:, :].rearrange("p (c t) -> p c t", t=64)
            else:
                outA = arv[:, 0, 0, 0:n]
                outR = arv[:, 0, 1, 0:n]
                inG = GPRV[p][:, :n]; inK = KKn[p][:, :n]; inR = Rr[p][:, :n]; inGm = GAM[p][:, :n]
            P.I("dve", "scalar_tensor_tensor", outA, inG, -1.0, inK, ALU.mult, ALU.mult, reads=[GPRV[p], KKn[p]], writes=[AR[p]])
            P.I("pool", "tensor_tensor", outR, inR, inGm, ALU.mult, reads=[Rr[p], GAM[p]], writes=[AR[p]])
            P.I("pool", "tensor_tensor", ta[:, :n], KKn[p][:, :n], AA[p][:, :n], ALU.mult, reads=[KKn[p], AA[p]], writes=[ta])
            P.I("dve", "tensor_tensor", BF[p][:, :n], ta[:, :n], GINV[p][:, :n], ALU.mult, reads=[ta, GINV[p]], writes=[BF[p]])
            P.I("pool", "tensor_tensor", KF[p][:, :n], KM[p][:, :n], GINV[p][:, :n], ALU.mult, reads=[KM[p], GINV[p]], writes=[KF[p]])
            P.I("dve", "scalar_tensor_tensor", RKB[p][:, :n], Rr[p][:, :n], PV[:, p, 7:8], KM[p][:, :n], ALU.mult, ALU.mult, reads=[Rr[p], PV, KM[p]], writes=[RKB[p]])
        chunks = [(c, min(64, n_sc - 64 * c)) for c in range(nchunk)]
        prev = None
        for it in chunks + [None]:
            streams = []
            if it is not None:
                s_ = chunk_counter % 2
                for p in range(2):
                    if debug != 1:
                        streams.append(pre_stream(p, s_, it[0], it[1]))
            if prev is not None and debug == 0:
                for p in range(2):
                    streams.append(chain_stream(p, prev[2], prev[0], prev[1], sti[p]))
                    sti[p] = 1 - sti[p]
            run_streams(streams)
            if it is not None:
                prev = (it[0], it[1], s_)
                chunk_counter += 1
            else:
                prev = None
        for p in range(2):
            if debug != 0:
                P.I("dve", "tensor_copy", YS[p][:, :n], BF[p][:, :n], reads=[BF[p]], writes=[YS[p]])
            ps = nxt("pp", pp)
            P.I("pe", "matmul", ps[:, :n], BLKM[:], YS[p][:, :n], start=True, stop=True, reads=[BLKM, YS[p]], writes=[ps])
            P.I("dve", "tensor_tensor", YC[p][:, :n], YS[p][:, :n], ps[:, :n], ALU.subtract, reads=[YS[p], ps], writes=[YC[p]])
            ta, tb, tc = TA[0], TA[1], TA[2]
            P.I("act", "activation", ta[:, :n], YC[p][:, :n], AF.Square, reads=[YC[p]], writes=[ta])
            ps = nxt("pp", pp)
            P.I("pe", "matmul", ps[:, :n], BLKM[:], ta[:, :n], start=True, stop=True, reads=[BLKM, ta], writes=[ps])
            P.I("act", "activation", tb[:, :n], ps[:, :n], AF.Sqrt, bias=epsg[:, 0:1], scale=1.0, reads=[ps, epsg], writes=[tb])
            P.I("dve", "reciprocal", tb[:, :n], tb[:, :n], reads=[tb], writes=[tb])
            P.I("dve", "tensor_tensor", YC[p][:, :n], YC[p][:, :n], tb[:, :n], ALU.mult, reads=[YC[p], tb], writes=[YC[p]])
            P.I("dve", "tensor_scalar", YC[p][:, :n], YC[p][:, :n], PV[:, p, 8:9], PV[:, p, 9:10], ALU.mult, ALU.add, reads=[YC[p], PV], writes=[YC[p]])
            ps = nxt("pp", pp)
            P.I("pe", "matmul", ps[:, :n], BLK1[:], RKB[p][:, :n], start=True, stop=True, reads=[BLK1, RKB[p]], writes=[ps])
            P.I("dve", "tensor_tensor", tc[:, :n], Vv[p][:, :n], ps[:, :n], ALU.mult, reads=[Vv[p], ps], writes=[tc])
            P.I("pool", "tensor_tensor", YC[p][:, :n], YC[p][:, :n], tc[:, :n], ALU.add, reads=[YC[p], tc], writes=[YC[p]])
            P.I("pool", "tensor_tensor", YS[p][:, :n], YC[p][:, :n], GT[p][:, :n], ALU.mult, reads=[YC[p], GT[p]], writes=[YS[p]])
            P.dma("sp", ydT[p * 128:(p + 1) * 128, t0:t0 + n], YS[p][:, :n], reads=[YS[p]], writes=[ydT])
    for p in range(2):
        P.dma("sp", st_out[p], ST[p][sti[p]][:], reads=[ST[p][sti[p]]], writes=[st_out])
    return P.finish([ydT, st_out])


import math

TWO_PI = 2.0 * math.pi


def build_s5(T, L):
    P = Prog()
    assert T % L == 0 and L <= 256
    NCH = T // L
    uT = P.dram("uT", [256, T], F32, "ExternalInput")
    s1 = P.dram("s1", [128, 3, 16], F32, "ExternalInput")
    l2 = P.dram("l2", [128, 2, 4, 64], F32, "ExternalInput")
    l2dt = P.dram("l2dt", [128, 2], F32, "ExternalInput")
    cs = P.dram("cs", [128, 2, 16, 16], F32, "ExternalInput")
    dsk = P.dram("dsk", [128, 2], F32, "ExternalInput")
    bmask = P.dram("bmask", [128, 8], F32, "ExternalInput")
    ycT = P.dram("ycT", [256, T], F32, "ExternalOutput")

    cnt = {}

    def nxt(key, lst):
        i = cnt.get(key, 0)
        cnt[key] = i + 1
        return lst[i % len(lst)]

    S1 = P.sbuf("S1", [128, 3, 16], F32); P.dma("sp", S1[:], s1[:], writes=[S1])
    L2 = P.sbuf("L2", [128, 2, 4, 64], F32); P.dma("sp", L2[:], l2[:], writes=[L2])
    L2DT = P.sbuf("L2DT", [128, 2], F32); P.dma("sp", L2DT[:], l2dt[:], writes=[L2DT])
    CS = P.sbuf("CS", [128, 2, 16, 16], F32); P.dma("sp", CS[:], cs[:], writes=[CS])
    DSK = P.sbuf("DSK", [128, 2], F32); P.dma("sp", DSK[:], dsk[:], writes=[DSK])
    BMK = P.sbuf("BMK", [128, 8], F32); P.dma("sp", BMK[:], bmask[:], writes=[BMK])

    IDENT = P.sbuf("IDENT", [128, 128], F32)
    JM = P.sbuf("JM", [128, 128], F32)
    P.I("pool", "memset", IDENT[:], 0.0, writes=[IDENT])
    P.I("pool", "affine_select", IDENT[:], IDENT[:], pattern=[[-1, 128]], compare_op=ALU.not_equal, fill=1.0, base=0, channel_multiplier=1, reads=[IDENT], writes=[IDENT])
    P.I("pool", "memset", JM[:], 0.0, writes=[JM])
    P.I("pool", "affine_select", JM[:], JM[:], pattern=[[1, 128]], compare_op=ALU.not_equal, fill=1.0, base=-64, channel_multiplier=-1, reads=[JM], writes=[JM])
    P.I("pool", "affine_select", JM[:], JM[:], pattern=[[-1, 128]], compare_op=ALU.not_equal, fill=-1.0, base=-64, channel_multiplier=1, reads=[JM], writes=[JM])

    def sincos(ang, np_, shape_tail, tag):
        shp = [128] + list(shape_tail)
        ki = P.sbuf(f"ki_{tag}", shp, I32)
        kf = P.sbuf(f"kf_{tag}", shp, F32)
        a2 = P.sbuf(f"a2_{tag}", shp, F32)
        sn = P.sbuf(f"sn_{tag}", shp, F32)
        cn = P.sbuf(f"cn_{tag}", shp, F32)
        for (dst, shift) in ((sn, 0.0), (cn, math.pi / 2)):
            if shift:
                P.I("dve", "tensor_scalar", a2[:np_], ang[:np_], shift, None, ALU.add, reads=[ang], writes=[a2])
                src = a2
            else:
                src = ang
            P.I("dve", "tensor_scalar", ki[:np_], src[:np_], 1.0 / TWO_PI, None, ALU.mult, reads=[src], writes=[ki])
            P.I("dve", "tensor_copy", kf[:np_], ki[:np_], reads=[ki], writes=[kf])
            P.I("dve", "scalar_tensor_tensor", kf[:np_], kf[:np_], -TWO_PI, src[:np_], ALU.mult, ALU.add, reads=[kf, src], writes=[kf])
            P.I("dve", "tensor_scalar", kf[:np_], kf[:np_], math.pi, -math.pi, ALU.min, ALU.max, reads=[kf], writes=[kf])
            P.I("act", "activation", dst[:np_], kf[:np_], AF.Sin, reads=[kf], writes=[dst])
        return sn, cn

    DT1 = P.sbuf("DT1", [128, 16], F32)
    MAG = P.sbuf("MAG", [128, 16], F32)
    TH = P.sbuf("TH", [128, 16], F32)
    THL = P.sbuf("THL", [128, 16], F32)
    P.I("act", "activation", DT1[:], S1[:, 2, :], AF.Exp, reads=[S1], writes=[DT1])
    P.I("dve", "tensor_tensor", MAG[:], S1[:, 0, :], DT1[:], ALU.mult, reads=[S1, DT1], writes=[MAG])
    P.I("act", "activation", MAG[:], MAG[:], AF.Exp, reads=[MAG], writes=[MAG])
    P.I("dve", "tensor_tensor", TH[:], S1[:, 1, :], DT1[:], ALU.mult, reads=[S1, DT1], writes=[TH])
    P.I("dve", "tensor_scalar", THL[:], TH[:], float(L), None, ALU.mult, reads=[TH], writes=[THL])
    sinL, cosL = sincos(THL, 128, [16], "L")
    ROT = P.sbuf("ROT", [128, 16, 128], F32)
    for g in range(16):
        P.I("dve", "tensor_scalar", ROT[:, g, :], IDENT[:], cosL[:, g:g + 1], None, ALU.mult, reads=[IDENT, cosL], writes=[ROT])
        P.I("dve", "scalar_tensor_tensor", ROT[:, g, :], JM[:], sinL[:, g:g + 1], ROT[:, g, :], ALU.mult, ALU.add, reads=[JM, sinL, ROT], writes=[ROT])
    IOTA = P.sbuf("IOTA", [128, L], F32)
    P.I("pool", "iota", IOTA[:], [[1, L]], base=0, channel_multiplier=0, allow_small_or_imprecise_dtypes=True, writes=[IOTA])
    ANG = P.sbuf("ANG", [128, 16, L], F32)
    for g in range(16):
        P.I("dve", "tensor_scalar", ANG[:, g, :], IOTA[:], TH[:, g:g + 1], None, ALU.mult, reads=[IOTA, TH], writes=[ANG])
    SIN, COS = sincos(ANG, 128, [16, L], "T")

    DT2 = P.sbuf("DT2", [128, 2], F32)
    P.I("act", "activation", DT2[:], L2DT[:], AF.Exp, reads=[L2DT], writes=[DT2])
    MG2 = P.sbuf("MG2", [128, 2, 64], F32)
    TH2 = P.sbuf("TH2", [128, 2, 64], F32)
    for j in range(2):
        P.I("act", "activation", MG2[:, j, :], L2[:, j, 0, :], AF.Exp, scale=DT2[:, j:j + 1], reads=[L2, DT2], writes=[MG2])
        P.I("dve", "tensor_scalar", TH2[:, j, :], L2[:, j, 1, :], DT2[:, j:j + 1], None, ALU.mult, reads=[L2, DT2], writes=[TH2])
    sn2, cn2 = sincos(TH2, 128, [2, 64], "B")
    AR1 = P.sbuf("AR1", [128, 2, 64], F32)
    AI = P.sbuf("AI", [128, 2, 64], F32)
    DEN = P.sbuf("DEN", [128, 2, 64], F32)
    T1 = P.sbuf("T1", [128, 2, 64], F32)
    CR = P.sbuf("CR", [128, 2, 64], F32)
    CI = P.sbuf("CI", [128, 2, 64], F32)
    LR = L2[:, :, 0, :]
    LI = L2[:, :, 1, :]
    BRE = L2[:, :, 2, :]
    BIM = L2[:, :, 3, :]
    P.I("dve", "tensor_tensor", AR1[:], MG2[:], cn2[:], ALU.mult, reads=[MG2, cn2], writes=[AR1])
    P.I("dve", "tensor_scalar", AR1[:], AR1[:], -1.0, None, ALU.add, reads=[AR1], writes=[AR1])
    P.I("dve", "tensor_tensor", AI[:], MG2[:], sn2[:], ALU.mult, reads=[MG2, sn2], writes=[AI])
    P.I("dve", "tensor_tensor", DEN[:], LR, LR, ALU.mult, reads=[L2], writes=[DEN])
    P.I("dve", "tensor_tensor", T1[:], LI, LI, ALU.mult, reads=[L2], writes=[T1])
    P.I("dve", "tensor_tensor", DEN[:], DEN[:], T1[:], ALU.add, reads=[DEN, T1], writes=[DEN])
    P.I("dve", "reciprocal", DEN[:], DEN[:], reads=[DEN], writes=[DEN])
    P.I("dve", "tensor_tensor", CR[:], AR1[:], LR, ALU.mult, reads=[AR1, L2], writes=[CR])
    P.I("dve", "tensor_tensor", T1[:], AI[:], LI, ALU.mult, reads=[AI, L2], writes=[T1])
    P.I("dve", "tensor_tensor", CR[:], CR[:], T1[:], ALU.add, reads=[CR, T1], writes=[CR])
    P.I("dve", "tensor_tensor", CR[:], CR[:], DEN[:], ALU.mult, reads=[CR, DEN], writes=[CR])
    P.I("dve", "tensor_tensor", CI[:], AI[:], LR, ALU.mult, reads=[AI, L2], writes=[CI])
    P.I("dve", "tensor_tensor", T1[:], AR1[:], LI, ALU.mult, reads=[AR1, L2], writes=[T1])
    P.I("dve", "tensor_tensor", CI[:], CI[:], T1[:], ALU.subtract, reads=[CI, T1], writes=[CI])
    P.I("dve", "tensor_tensor", CI[:], CI[:], DEN[:], ALU.mult, reads=[CI, DEN], writes=[CI])
    BB1 = P.sbuf("BB1", [128, 2, 2, 64], F32)
    BB2 = P.sbuf("BB2", [128, 2, 2, 64], F32)
    P.I("dve", "tensor_tensor", BB1[:, :, 0, :], CR[:], BRE, ALU.mult, reads=[CR, L2], writes=[BB1])
    P.I("dve", "tensor_tensor", T1[:], CI[:], BIM, ALU.mult, reads=[CI, L2], writes=[T1])
    P.I("dve", "tensor_tensor", BB1[:, :, 0, :], BB1[:, :, 0, :], T1[:], ALU.subtract, reads=[BB1, T1], writes=[BB1])
    P.I("dve", "tensor_tensor", BB1[:, :, 1, :], CR[:], BIM, ALU.mult, reads=[CR, L2], writes=[BB1])
    P.I("dve", "tensor_tensor", T1[:], CI[:], BRE, ALU.mult, reads=[CI, L2], writes=[T1])
    P.I("dve", "tensor_tensor", BB1[:, :, 1, :], BB1[:, :, 1, :], T1[:], ALU.add, reads=[BB1, T1], writes=[BB1])
    P.I("dve", "tensor_copy", BB2[:, :, 0, :], BB1[:, :, 1, :], reads=[BB1], writes=[BB2])
    P.I("dve", "tensor_scalar", BB2[:, :, 1, :], BB1[:, :, 0, :], -1.0, None, ALU.mult, reads=[BB1], writes=[BB2])
    BM = P.sbuf("BM", [128, 16, 2, 128], BF16)
    for g in range(16):
        j, g8 = g // 8, g % 8
        P.I("dve", "tensor_scalar", BM[:, g, 0, :], BB1[:, j, :, :].rearrange("p a b -> p (a b)"), BMK[:, g8:g8 + 1], None, ALU.mult, reads=[BB1, BMK], writes=[BM])
        P.I("dve", "tensor_scalar", BM[:, g, 1, :], BB2[:, j, :, :].rearrange("p a b -> p (a b)"), BMK[:, g8:g8 + 1], None, ALU.mult, reads=[BB2, BMK], writes=[BM])
    CM = P.sbuf("CM", [128, 16, 2, 128], BF16)
    P.I("pool", "memset", CM[:], 0.0, writes=[CM])
    for g in range(16):
        g8 = g % 8
        band = slice(16 * g8, 16 * g8 + 16)
        P.I("dve", "tensor_copy", CM[0:64, g, 0, band], CS[0:64, 0, g, :], reads=[CS, CM], writes=[CM])
        P.I("dve", "tensor_scalar", CM[64:128, g, 0, band], CS[64:128, 1, g, :], -1.0, None, ALU.mult, reads=[CS, CM], writes=[CM])
        P.I("dve", "tensor_scalar", CM[0:64, g, 1, band], CS[0:64, 1, g, :], -1.0, None, ALU.mult, reads=[CS, CM], writes=[CM])
        P.I("dve", "tensor_scalar", CM[64:128, g, 1, band], CS[64:128, 0, g, :], -1.0, None, ALU.mult, reads=[CS, CM], writes=[CM])

    px = [P.psum(f"px{i}", [128, 2, 256]) for i in range(3)]
    pi = P.psum("pi", [128, 512])
    py = [P.psum(f"py{i}", [128, 512]) for i in range(4)]
    UB = [[P.sbuf(f"UB{j}{i}", [128, L], BF16) for i in range(2)] for j in range(2)]
    UF = [[P.sbuf(f"UF{j}{i}", [128, L], F32) for i in range(2)] for j in range(2)]
    TT1 = [P.sbuf(f"TT1{i}", [128, L], F32) for i in range(3)]
    TT2 = [P.sbuf(f"TT2{i}", [128, L], F32) for i in range(3)]
    RIN = [P.sbuf(f"RIN{i}", [128, L], F32) for i in range(3)]
    GG = [P.sbuf(f"GG{i}", [128, L], F32) for i in range(3)]
    GC = [P.sbuf(f"GC{i}", [128, L], BF16) for i in range(3)]
    GS = [P.sbuf(f"GS{i}", [128, L], BF16) for i in range(3)]
    GLs = [P.sbuf(f"GLs{i}", [128, 16], F32) for i in range(2)]
    INIT = [P.sbuf(f"INIT{i}", [128, 16], F32) for i in range(2)]
    YY = [P.sbuf(f"YY{i}", [128, L], F32) for i in range(2)]
    GW = [P.sbuf(f"GW{i}", [128, L], F32) for i in range(2)]
    GL2 = [P.sbuf(f"GL2{i}", [128, L], F32) for i in range(2)]

    for ci in range(NCH):
        t0 = ci * L
        ub = [nxt(f"UB{j}", UB[j]) for j in range(2)]
        uf = [nxt(f"UF{j}", UF[j]) for j in range(2)]
        for j in range(2):
            P.dma("pool", ub[j][:, :], uT[j * 128:(j + 1) * 128, t0:t0 + L], writes=[ub[j]])
            P.dma("sp", uf[j][:, :], uT[j * 128:(j + 1) * 128, t0:t0 + L], writes=[uf[j]])
        gl_prev = GLs[(ci + 1) % 2]
        gl_cur = GLs[ci % 2]
        init = INIT[ci % 2]
        if ci > 0:
            for g in range(16):
                P.I("pe", "matmul", pi[:, g:g + 1], ROT[:, g, :], gl_prev[:, g:g + 1], start=True, stop=True, reads=[ROT, gl_prev], writes=[pi])
            P.I("act", "copy", init[:], pi[:, 0:16], reads=[pi], writes=[init])
        for g in range(16):
            j, g8 = g // 8, g % 8
            pxt = nxt("px", px)
            P.I("pe", "matmul", pxt[:, 0, :L], BM[:, g, 0, :], ub[j][:, :], start=True, stop=True, reads=[BM, ub[j]], writes=[pxt])
            P.I("pe", "matmul", pxt[:, 1, :L], BM[:, g, 1, :], ub[j][:, :], start=True, stop=True, reads=[BM, ub[j]], writes=[pxt])
            t1 = nxt("TT1", TT1); t2 = nxt("TT2", TT2); rin = nxt("RIN", RIN); gg = nxt("GG", GG); gc = nxt("GC", GC); gs = nxt("GS", GS)
            P.I("dve", "tensor_tensor", t1[:], pxt[:, 0, :L], COS[:, g, :], ALU.mult, reads=[pxt, COS], writes=[t1])
            P.I("dve", "tensor_tensor", t2[:], pxt[:, 1, :L], SIN[:, g, :], ALU.mult, reads=[pxt, SIN], writes=[t2])
            P.I("pool", "tensor_tensor", rin[:], t1[:], t2[:], ALU.add, reads=[t1, t2], writes=[rin])
            ini = 0.0 if ci == 0 else init[:, g:g + 1]
            P.I("dve", "tensor_tensor_scan", gg[:], MAG[:, g:g + 1].to_broadcast([128, L]), rin[:], ini, ALU.mult, ALU.add, reads=[MAG, rin] + ([init] if ci > 0 else []), writes=[gg])
            P.I("act", "copy", gl_cur[:, g:g + 1], gg[:, L - 1:L], reads=[gg], writes=[gl_cur])
            P.I("pool", "tensor_tensor", gc[:], gg[:], COS[:, g, :], ALU.mult, reads=[gg, COS], writes=[gc])
            P.I("pool", "tensor_tensor", gs[:], gg[:], SIN[:, g, :], ALU.mult, reads=[gg, SIN], writes=[gs])
            pyt = py[(2 * ci + j) % 4]
            P.I("pe", "matmul", pyt[:, :L], CM[:, g, 0, :], gc[:], start=(g8 == 0), stop=False, reads=[CM, gc], writes=[pyt])
            P.I("pe", "matmul", pyt[:, :L], CM[:, g, 1, :], gs[:], start=False, stop=(g8 == 7), reads=[CM, gs], writes=[pyt])
            if g8 == 7:
                yy = nxt("YY", YY); gw = nxt("GW", GW); gl2 = nxt("GL2", GL2)
                P.I("dve", "scalar_tensor_tensor", yy[:], uf[j][:], DSK[:, j:j + 1], pyt[:, :L], ALU.mult, ALU.add, reads=[uf[j], DSK, pyt], writes=[yy])
                P.I("act", "activation", gw[:], yy[:], AF.Square, reads=[yy], writes=[gw])
                P.I("pool", "tensor_scalar", gw[:], gw[:], 0.044715, 1.0, ALU.mult, ALU.add, reads=[gw], writes=[gw])
                P.I("pool", "tensor_tensor", gw[:], gw[:], yy[:], ALU.mult, reads=[gw, yy], writes=[gw])
                P.I("act", "activation", gl2[:], gw[:], AF.Sigmoid, scale=1.5957691216057308, reads=[gw], writes=[gl2])
                P.I("pool", "tensor_tensor", gl2[:], gl2[:], yy[:], ALU.mult, reads=[gl2, yy], writes=[gl2])
                P.dma("sp", ycT[j * 128:(j + 1) * 128, t0:t0 + L], gl2[:], reads=[gl2], writes=[ycT])
    return P.finish([ycT])


T_FULL = 8208
SEQ = 8192
NB_ = 2
S5_L = 228
_PROGS = {}


def _prog(key):
    if key not in _PROGS:
        if key == "linA":
            _PROGS[key] = build_lin(False, 5120, False, False)
        elif key == "linC_even":
            _PROGS[key] = build_lin(True, 4544, False, False)
        elif key == "linC_odd":
            _PROGS[key] = build_lin(True, 5120, False, True)
        elif key == "linC_final":
            _PROGS[key] = build_lin(True, 0, True, True)
        elif key == "even":
            _PROGS[key] = build_even(T_FULL)
        elif key == "rwkvA":
            _PROGS[key] = build_rwkv(T_FULL, meta=True)
        elif key == "rwkvB":
            _PROGS[key] = build_rwkv(2048, meta=False)
        elif key == "s5":
            _PROGS[key] = build_s5(T_FULL, S5_L)
    return _PROGS[key]


def _run(key, in_maps):
    nc = _prog(key)
    res = run_bass_kernel_spmd(nc, in_maps, core_ids=list(range(8)))
    return res.results


def _colmajor(v, n):
    return np.ascontiguousarray(np.asarray(v, np.float32).reshape(n, 128).T)


def _tok_shard(full_T, b, q):
    return np.ascontiguousarray(np.concatenate([full_T[:, 0:16], full_T[:, 16 + 2048 * q:16 + 2048 * (q + 1)]], axis=1))


def _assemble_T(shards, b):
    parts = [shards[4 * b][:, 0:16]] + [shards[4 * b + q][:, 16:] for q in range(4)]
    return np.concatenate(parts, axis=1)


def _dtab(T):
    NQ = (T - 16) // 256
    NB = 2 + NQ + 2 * NQ + 2
    d = [0.0] + [float(16 + 256 * i) for i in range(NQ)] + [128.0 * k for k in range(-1, 2 * NQ)]
    d = d + [0.0] * (NB - len(d))
    return np.array(d, np.float32)[None]


def _prep_s5(lam_re, lam_im, log_dt, b_re, b_im, c_re, c_im, d):
    G = lam_re.shape[0]
    s1 = np.zeros((128, 3, G), np.float32)
    s1[:, 0, :] = np.concatenate([lam_re.T, lam_re.T], 0)
    s1[:, 1, :] = np.concatenate([lam_im.T, lam_im.T], 0)
    s1[:, 2, :] = log_dt[None, :]
    l2 = np.zeros((128, 2, 4, 64), np.float32)
    l2dt = np.zeros((128, 2), np.float32)
    for j in range(2):
        gs = slice(8 * j, 8 * j + 8)
        l2[:, j, 0, :] = np.repeat(lam_re[gs], 16, axis=0)
        l2[:, j, 1, :] = np.repeat(lam_im[gs], 16, axis=0)
        l2[:, j, 2, :] = b_re[gs].transpose(0, 2, 1).reshape(128, 64)
        l2[:, j, 3, :] = b_im[gs].transpose(0, 2, 1).reshape(128, 64)
        l2dt[:, j] = np.repeat(log_dt[gs], 16)
    cs = np.zeros((128, 2, G, 16), np.float32)
    cr = c_re.transpose(2, 0, 1)
    ci = c_im.transpose(2, 0, 1)
    cs[:, 0] = np.concatenate([cr, cr], 0)
    cs[:, 1] = np.concatenate([ci, ci], 0)
    dsk = np.ascontiguousarray(d.reshape(2, 128).T)
    bmask = np.zeros((128, 8), np.float32)
    for g8 in range(8):
        bmask[16 * g8:16 * g8 + 16, g8] = 1
    return {"s1": s1, "l2": l2, "l2dt": l2dt, "cs": cs, "dsk": dsk, "bmask": bmask}


def kernel(**inp):
    f32 = np.float32
    inp = {k: np.asarray(v, f32) for k, v in inp.items()}
    x = inp["x"]
    meta = inp["meta_tokens"]
    hT = []
    for c in range(8):
        b, q = c // 4, c % 4
        tok = np.concatenate([meta, x[b, 2048 * q:2048 * (q + 1)]], axis=0)
        hT.append(np.ascontiguousarray(tok.T))
    g0 = np.concatenate([_colmajor(inp["norm_mix_g"][0], 16), _colmajor(inp["norm_mix_g"][0], 16)], axis=1)
    res = _run("linA", [{"hT": hT[c], "gvec": g0, "w_in": inp["ev_w_in"][0]} for c in range(8)])
    zT = [r["zT"] for r in res]
    out = None
    for layer in range(4):
        j = layer // 2
        ZT = [_assemble_T(zT, b) for b in range(NB_)]
        MIX = [np.zeros((2048, T_FULL), f32) for _ in range(NB_)]
        if layer % 2 == 0:
            linit = 0.8 - 0.6 * math.exp(-0.3 * layer)
            maps = []
            for c in range(8):
                b, q = c // 4, c % 4
                Z = ZT[b]
                lruvec = np.zeros((128, 2, 8), f32)
                for cc in range(2):
                    sl = slice(256 * q + 128 * cc, 256 * q + 128 * cc + 128)
                    for jj in range(4):
                        lruvec[:, cc, jj] = inp["ev_conv_w"][j][jj, sl]
                    lruvec[:, cc, 4] = inp["ev_conv_b"][j][sl]
                    lruvec[:, cc, 5] = inp["ev_lru_ba"][j][sl]
                    lruvec[:, cc, 6] = inp["ev_lru_bx"][j][sl]
                    lruvec[:, cc, 7] = inp["ev_lru_lambda"][j][sl]
                heads = [2 * q, 2 * q + 1]
                maps.append({
                    "xaT": np.ascontiguousarray(Z[256 * q:256 * q + 256]),
                    "gaT": np.ascontiguousarray(Z[1024 + 256 * q:1024 + 256 * q + 256]),
                    "lruvec": lruvec,
                    "wa": np.ascontiguousarray(inp["ev_lru_wa"][j][2 * q:2 * q + 2]),
                    "wx": np.ascontiguousarray(inp["ev_lru_wx"][j][2 * q:2 * q + 2]),
                    "qT": np.ascontiguousarray(np.stack([Z[2048 + h * 128:2048 + h * 128 + 128] for h in heads])),
                    "kT": np.ascontiguousarray(np.stack([Z[3072 + h * 128:3072 + h * 128 + 128] for h in heads])),
                    "vv": np.ascontiguousarray(np.stack([Z[4096 + h * 128:4096 + h * 128 + 128].T for h in heads])),
                    "lqk": np.concatenate([inp["ev_lq1"][j], inp["ev_lk1"][j], inp["ev_lq2"][j], inp["ev_lk2"][j]])[None].astype(f32),
                    "sublng": inp["ev_subln_g"][j][None],
                    "cst": np.array([[2.0 ** -(heads[0] + 1), 2.0 ** -(heads[1] + 1), linit, 1.0 - linit, 0, 0, 0, 0]], f32),
                    "dtab": _dtab(T_FULL),
                })
            res = _run("even", maps)
            for c in range(8):
                b, q = c // 4, c % 4
                MIX[b][256 * q:256 * q + 256] = res[c]["yaT"]
                for hl in range(2):
                    h = 2 * q + hl
                    MIX[b][1024 + h * 128:1024 + h * 128 + 128] = res[c]["yb"][hl].T
        else:
            maps = []
            for c in range(8):
                b, q = c // 4, c % 4
                gs = slice(16 * q, 16 * q + 16)
                m = {"uT": np.ascontiguousarray(ZT[b][256 * q:256 * q + 256])}
                m.update(_prep_s5(inp["od_s5_lam_re"][j][gs], inp["od_s5_lam_im"][j][gs], inp["od_s5_log_dt"][j][gs],
                                  inp["od_s5_b_re"][j][gs], inp["od_s5_b_im"][j][gs], inp["od_s5_c_re"][j][gs], inp["od_s5_c_im"][j][gs],
                                  inp["od_s5_d"][j][256 * q:256 * q + 256]))
                maps.append(m)
            res = _run("s5", maps)
            for c in range(8):
                b, q = c // 4, c % 4
                MIX[b][256 * q:256 * q + 256] = res[c]["ycT"]
            mu = inp["od_rw_mu"][j]
            maps = []
            for c in range(8):
                b, q = c // 4, c % 4
                Z = ZT[b]
                pv = np.zeros((128, 2, 12), f32)
                for p in range(2):
                    sl = slice(256 * q + 128 * p, 256 * q + 128 * p + 128)
                    pv[:, p, 0] = mu[0:1024][sl]
                    pv[:, p, 1] = mu[1024:2048][sl]
                    pv[:, p, 2] = mu[2048:3072][sl]
                    pv[:, p, 3] = inp["od_rw_w0"][j][sl]
                    pv[:, p, 4] = inp["od_rw_a0"][j][sl]
                    pv[:, p, 5] = inp["od_rw_kk"][j][sl]
                    pv[:, p, 6] = inp["od_rw_ka"][j][sl]
                    pv[:, p, 7] = inp["od_rw_rk"][j][sl]
                    pv[:, p, 8] = inp["od_rw_ln_w"][j][sl]
                    pv[:, p, 9] = inp["od_rw_ln_b"][j][sl]
                pl = np.zeros((128, 4), f32)
                pl[:96, 0] = mu[3072:3168]
                pl[:96, 1] = mu[3168:3264]
                pl[:, 2] = mu[3264:3392]
                pl[:, 3] = mu[3392:3520]
                cs_ = slice(256 * q, 256 * q + 256)
                def halo(a):
                    return np.concatenate([np.zeros((a.shape[0], 1), f32), a], axis=1)
                maps.append({
                    "rT": halo(Z[1024 + 256 * q:1024 + 256 * q + 256]),
                    "kT": halo(Z[2048 + 256 * q:2048 + 256 * q + 256]),
                    "vT": halo(Z[3072 + 256 * q:3072 + 256 * q + 256]),
                    "wlT": halo(Z[4096:4192]), "alT": halo(Z[4192:4288]), "glT": halo(Z[4288:4544]),
                    "pv": pv, "pl": pl,
                    "w2": np.ascontiguousarray(inp["od_rw_w2"][j][:, cs_]), "a2": np.ascontiguousarray(inp["od_rw_a2"][j][:, cs_]),
                    "g2": np.ascontiguousarray(inp["od_rw_g2"][j][:, cs_]),
                })
            st = [np.zeros((2, 128, 64), f32) for _ in range(8)]
            pos = 0
            while pos < T_FULL:
                n_ = T_FULL
                segmaps = []
                for c in range(8):
                    m = dict(maps[c])
                    for nm in ("rT", "kT", "vT", "wlT", "alT", "glT"):
                        m[nm] = np.ascontiguousarray(maps[c][nm][:, pos:pos + n_ + 1])
                    m["st_in"] = st[c]
                    segmaps.append(m)
                res = _run("rwkvA" if pos == 0 else "rwkvB", segmaps)
                for c in range(8):
                    b, q = c // 4, c % 4
                    MIX[b][1024 + 256 * q:1024 + 256 * q + 256, pos:pos + n_] = res[c]["ydT"]
                    st[c] = res[c]["st_out"]
                pos += n_
        last = (layer == 3)
        if last:
            g2v = _colmajor(inp["final_norm_g"], 16)
        else:
            g2v = _colmajor(inp["norm_mix_g"][layer + 1], 16)
        gvec = np.concatenate([_colmajor(inp["norm_ffn_g"][layer], 16), g2v], axis=1)
        maps = []
        for c in range(8):
            b, q = c // 4, c % 4
            m = {"hT": hT[c], "mixT": _tok_shard(MIX[b], b, q), "gvec": gvec,
                 "w_out": inp["ev_w_out"][j] if layer % 2 == 0 else inp["od_w_out"][j],
                 "w_gate": inp["ffn_w_gate"][layer], "w_up": inp["ffn_w_up"][layer], "w_down": inp["ffn_w_down"][layer]}
            if layer % 2 == 1:
                m["glu_w"] = inp["od_glu_w"][j]
                m["glu_b"] = _colmajor(inp["od_glu_b"][j], 8)
            if not last:
                m["w_in"] = inp["od_w_in"][j] if layer % 2 == 0 else inp["ev_w_in"][j + 1]
            maps.append(m)
        key = "linC_final" if last else ("linC_even" if layer % 2 == 0 else "linC_odd")
        res = _run(key, maps)
        if last:
            out = np.zeros((NB_, SEQ, 2048), f32)
            for c in range(8):
                b, q = c // 4, c % 4
                out[b, 2048 * q:2048 * (q + 1)] = res[c]["outT"][:, 16:].T
        else:
            hT = [r["hT_out"] for r in res]
            zT = [r["zT"] for r in res]
    return out
```
